# Optimizing a Trainium2 kernel written in Bass

```python
import math
import jax, jax.numpy as jnp
from jax import lax
import numpy as np

D_MODEL = 2048
BATCH = 4
SEQ = 8192
DEPTH = 2
DEC_BATCH = 8
DEC_SEQ = 4096
PAST_LEN = 128

MIX_WIDTH = D_MODEL
POOL_WIDTH = MIX_WIDTH // 2
POOL_WINDOWS = (2, 4, 8, 16)
POOL_GROUPS = len(POOL_WINDOWS)
POOL_GROUP_WIDTH = POOL_WIDTH // POOL_GROUPS
ATTN_WIDTH = MIX_WIDTH - POOL_WIDTH
N_DIFF_HEADS = 8
DIFF_VDIM = ATTN_WIDTH // N_DIFF_HEADS
DIFF_QKDIM = DIFF_VDIM // 2
QK_WIDTH = N_DIFF_HEADS * 2 * DIFF_QKDIM
IN_WIDTH = 2 * QK_WIDTH + ATTN_WIDTH + POOL_WIDTH
N_MEM = 256
N_CROSS_HEADS = 4
CROSS_HDIM = D_MODEL // N_CROSS_HEADS
D_FF = 5632
CONV_WIDTH = 3
ROPE_THETA = 10000.0
EPS = 1e-6
Q_BLOCK = 128

kernel_name = 'hybrid_diffattn_pool_memory_encoder'


def rms_norm(x, g):
    xf = x.astype(jnp.float32)
    y = xf * lax.rsqrt(jnp.mean(xf * xf, axis=-1, keepdims=True) + EPS)
    return (y * g.astype(jnp.float32)).astype(x.dtype)


def rope(x):
    S, dh = x.shape[1], x.shape[-1]
    half = dh // 2
    inv = ROPE_THETA ** (-jnp.arange(half, dtype=jnp.float32) / half)
    ang = jnp.arange(S, dtype=jnp.float32)[:, None] * inv[None, :]
    shp = (S,) + (1,) * (x.ndim - 3) + (half,)
    cos = jnp.cos(ang).reshape(shp)
    sin = jnp.sin(ang).reshape(shp)
    xf = x.astype(jnp.float32)
    x1, x2 = xf[..., :half], xf[..., half:]
    out = jnp.concatenate([x1 * cos - x2 * sin, x2 * cos + x1 * sin], axis=-1)
    return out.astype(x.dtype)


def multiscale_pool(u, w_pool, pool_scale):
    B, S, _ = u.shape
    ug = u.reshape(B, S, POOL_GROUPS, POOL_GROUP_WIDTH)
    uf = ug.astype(jnp.float32)
    c = jnp.pad(jnp.cumsum(uf, axis=1), ((0, 0), (1, 0), (0, 0), (0, 0)))
    t = jnp.arange(S)
    outs = []
    for g, w in enumerate(POOL_WINDOWS):
        lo = jnp.clip(t - w // 2, 0, S)
        hi = jnp.clip(t + w // 2, 0, S)
        cg = c[:, :, g]
        win_sum = jnp.take(cg, hi, axis=1) - jnp.take(cg, lo, axis=1)
        mean = win_sum / (hi - lo).astype(jnp.float32)[None, :, None]
        outs.append(mean - uf[:, :, g])
    z = jnp.stack(outs, axis=2).astype(u.dtype)
    y = jnp.einsum('bsgc,gcd->bsgd', z, w_pool).reshape(B, S, POOL_WIDTH)
    return y * pool_scale


def diff_attention(q, k, v, lam, lam_init, g_sub):
    B, S = q.shape[0], q.shape[1]
    nb = S // Q_BLOCK
    scale = DIFF_QKDIM ** -0.5
    qb = q.reshape(B, nb, Q_BLOCK, N_DIFF_HEADS, 2, DIFF_QKDIM).transpose(1, 0, 2, 3, 4, 5)

    def block(qblk):
        s = jnp.einsum('bqhcd,bkhcd->bhcqk', qblk, k).astype(jnp.float32) * scale
        p = jax.nn.softmax(s, axis=-1)
        a = p[:, :, 0] - lam * p[:, :, 1]
        return jnp.einsum('bhqk,bkhd->bqhd', a.astype(v.dtype), v)

    o = lax.map(block, qb)
    o = o.transpose(1, 0, 2, 3, 4).reshape(B, S, N_DIFF_HEADS, DIFF_VDIM)
    o = rms_norm(o, g_sub) * (1.0 - lam_init)
    return o.reshape(B, S, ATTN_WIDTH)


def memory_cross_attention(h, mem, g_mem, wc_q, wc_kv, gc_q, gc_k, wc_o):
    B, S, _ = h.shape
    m = rms_norm(mem, g_mem)
    q = (h @ wc_q).reshape(B, S, N_CROSS_HEADS, CROSS_HDIM)
    kv = (m @ wc_kv).reshape(B, N_MEM, 2, N_CROSS_HEADS, CROSS_HDIM)
    q = rms_norm(q, gc_q)
    k = rms_norm(kv[:, :, 0], gc_k)
    v = kv[:, :, 1]
    s = jnp.einsum('bqhd,bkhd->bhqk', q, k).astype(jnp.float32) * (CROSS_HDIM ** -0.5)
    p = jax.nn.softmax(s, axis=-1)
    o = jnp.einsum('bhqk,bkhd->bqhd', p.astype(v.dtype), v).reshape(B, S, D_MODEL)
    return o @ wc_o


def conv_glu_ffn(h, w_up, conv_w, conv_b, w_down):
    u = h @ w_up
    up = jnp.pad(u, ((0, 0), (1, 1), (0, 0)))
    u = up[:, :-2] * conv_w[0] + up[:, 1:-1] * conv_w[1] + up[:, 2:] * conv_w[2] + conv_b
    gate, val = u[..., :D_FF], u[..., D_FF:]
    return (jax.nn.gelu(gate) * val) @ w_down


def run_trunk(x, mem, g_mix, w_in, g_q, g_k, lam_q1, lam_k1, lam_q2, lam_k2, g_sub, w_pool, pool_scale, w_out,
              g_cross, g_mem, wc_q, wc_kv, gc_q, gc_k, wc_o, g_ffn, w_up, conv_w, conv_b, w_down):
    B, S, _ = x.shape
    for l in range(DEPTH):
        lam_init = 0.8 - 0.6 * math.exp(-0.3 * l)
        lam = (jnp.exp(jnp.sum(lam_q1[l].astype(jnp.float32) * lam_k1[l].astype(jnp.float32)))
               - jnp.exp(jnp.sum(lam_q2[l].astype(jnp.float32) * lam_k2[l].astype(jnp.float32))) + lam_init)
        h = rms_norm(x, g_mix[l])
        z = h @ w_in[l]
        q = z[..., :QK_WIDTH].reshape(B, S, N_DIFF_HEADS, 2, DIFF_QKDIM)
        k = z[..., QK_WIDTH:2 * QK_WIDTH].reshape(B, S, N_DIFF_HEADS, 2, DIFF_QKDIM)
        v = z[..., 2 * QK_WIDTH:2 * QK_WIDTH + ATTN_WIDTH].reshape(B, S, N_DIFF_HEADS, DIFF_VDIM)
        u = z[..., 2 * QK_WIDTH + ATTN_WIDTH:]
        q = rope(rms_norm(q, g_q[l]))
        k = rope(rms_norm(k, g_k[l]))
        a_out = diff_attention(q, k, v, lam, lam_init, g_sub[l])
        p_out = multiscale_pool(u, w_pool[l], pool_scale[l])
        x = x + jnp.concatenate([a_out, p_out], axis=-1) @ w_out[l]
        x = x + memory_cross_attention(rms_norm(x, g_cross[l]), mem, g_mem[l], wc_q[l], wc_kv[l], gc_q[l], gc_k[l], wc_o[l])
        x = x + conv_glu_ffn(rms_norm(x, g_ffn[l]), w_up[l], conv_w[l], conv_b[l], w_down[l])
    return x


def setup_inputs(seed: int = 0) -> dict:
    key = jax.random.key(seed)
    ks = jax.random.split(key, 32)
    f32 = jnp.float32

    def nrm(k, shape, scale):
        return jax.random.normal(k, shape, f32) * scale

    def gain(k, shape):
        return 1.0 + 0.05 * jax.random.normal(k, shape, f32)

    L, D = DEPTH, D_MODEL
    return {
        'x_prompt': nrm(ks[0], (BATCH, SEQ, D), 1.0),
        'x_sample': nrm(ks[1], (DEC_BATCH, DEC_SEQ, D), 1.0),
        'mem_prompt': nrm(ks[2], (BATCH, N_MEM, D), 1.0),
        'mem_sample': nrm(ks[3], (DEC_BATCH, N_MEM, D), 1.0),
        'g_mix': gain(ks[4], (L, D)),
        'w_in': nrm(ks[5], (L, D, IN_WIDTH), D ** -0.5),
        'g_q': gain(ks[6], (L, DIFF_QKDIM)),
        'g_k': gain(ks[7], (L, DIFF_QKDIM)),
        'lam_q1': nrm(ks[8], (L, DIFF_QKDIM), 0.1),
        'lam_k1': nrm(ks[9], (L, DIFF_QKDIM), 0.1),
        'lam_q2': nrm(ks[10], (L, DIFF_QKDIM), 0.1),
        'lam_k2': nrm(ks[11], (L, DIFF_QKDIM), 0.1),
        'g_sub': gain(ks[12], (L, DIFF_VDIM)),
        'w_pool': nrm(ks[13], (L, POOL_GROUPS, POOL_GROUP_WIDTH, POOL_GROUP_WIDTH), POOL_GROUP_WIDTH ** -0.5),
        'pool_scale': gain(ks[14], (L, POOL_WIDTH)),
        'w_out': nrm(ks[15], (L, MIX_WIDTH, D), MIX_WIDTH ** -0.5),
        'g_cross': gain(ks[16], (L, D)),
        'g_mem': gain(ks[17], (L, D)),
        'wc_q': nrm(ks[18], (L, D, D), D ** -0.5),
        'wc_kv': nrm(ks[19], (L, D, 2 * D), D ** -0.5),
        'gc_q': gain(ks[20], (L, CROSS_HDIM)),
        'gc_k': gain(ks[21], (L, CROSS_HDIM)),
        'wc_o': nrm(ks[22], (L, D, D), D ** -0.5),
        'g_ffn': gain(ks[23], (L, D)),
        'w_up': nrm(ks[24], (L, D, 2 * D_FF), D ** -0.5),
        'conv_w': nrm(ks[25], (L, CONV_WIDTH, 2 * D_FF), CONV_WIDTH ** -0.5),
        'conv_b': nrm(ks[26], (L, 2 * D_FF), 0.02),
        'w_down': nrm(ks[27], (L, D_FF, D), D_FF ** -0.5),
    }


def reference(x_prompt, x_sample, mem_prompt, mem_sample, g_mix, w_in, g_q, g_k, lam_q1, lam_k1, lam_q2, lam_k2,
              g_sub, w_pool, pool_scale, w_out, g_cross, g_mem, wc_q, wc_kv, gc_q, gc_k, wc_o, g_ffn, w_up,
              conv_w, conv_b, w_down):
    y_prompt = run_trunk(x_prompt, mem_prompt, g_mix, w_in, g_q, g_k, lam_q1, lam_k1, lam_q2, lam_k2, g_sub,
                         w_pool, pool_scale, w_out, g_cross, g_mem, wc_q, wc_kv, gc_q, gc_k, wc_o, g_ffn, w_up,
                         conv_w, conv_b, w_down)
    y_sample = run_trunk(x_sample, mem_sample, g_mix, w_in, g_q, g_k, lam_q1, lam_k1, lam_q2, lam_k2, g_sub,
                         w_pool, pool_scale, w_out, g_cross, g_mem, wc_q, wc_kv, gc_q, gc_k, wc_o, g_ffn, w_up,
                         conv_w, conv_b, w_down)
    return (y_prompt, y_sample)
```

```python
import contextlib
import math
import numpy as np
import concourse.bass as bass
import concourse.mybir as mybir
from concourse.bass_utils import run_bass_kernel_spmd

F32 = mybir.dt.float32
BF16 = mybir.dt.bfloat16
AF = mybir.ActivationFunctionType
ALU = mybir.AluOpType
AX = mybir.AxisListType

D = 2048
NH = 8
N_MEM = 256
D_FF = 5632
EPS = 1e-6
NEG = -30000.0


class Op:
    __slots__ = ("eng", "fn", "dma", "deps", "inc", "sem", "val", "bar", "pos", "q")


class Prog:
    CE = ("pe", "act", "dve", "pool")
    ALL = ("pe", "act", "dve", "pool", "sp")

    def __init__(self):
        self.ops = []
        self.last_w = {}
        self.readers = {}
        self.stream = {e: [] for e in self.ALL}
        self.persist = set()

    def op(self, eng, fn, r=(), w=(), dma=False, bg=False):
        o = Op()
        o.eng, o.fn, o.dma, o.inc, o.sem, o.val, o.bar = eng, fn, dma, False, None, 0, None
        o.q = ("bg" if bg else eng) if dma else None
        idx = len(self.ops)
        deps = set()
        for k in r:
            lw = self.last_w.get(k)
            if lw is not None:
                deps.add(lw)
        for k in w:
            lw = self.last_w.get(k)
            if lw is not None:
                deps.add(lw)
            rd = self.readers.get(k)
            if rd:
                deps.update(rd.values())
        for k in r:
            rd = self.readers.setdefault(k, {})
            rd[("d", idx) if dma else eng] = idx
        for k in w:
            self.last_w[k] = idx
            self.readers[k] = {}
            if bg:
                self.persist.add(k)
        best = {}
        keep = []
        for d in deps:
            od = self.ops[d]
            if od.dma:
                keep.append(d)
            else:
                if od.eng == eng and eng == "pe":
                    continue
                if od.eng not in best or best[od.eng] < d:
                    best[od.eng] = d
        keep.extend(best.values())
        o.deps = keep
        for d in keep:
            self.ops[d].inc = True
        o.pos = len(self.stream[eng])
        self.ops.append(o)
        self.stream[eng].append(idx)
        return idx

    def barrier(self):
        lasts = []
        for e in self.ALL:
            for li in reversed(self.stream[e]):
                ol = self.ops[li]
                if ol.bar:
                    break
                if not ol.dma:
                    lasts.append(li)
                    break
        for e in self.ALL:
            o = Op()
            o.eng, o.fn, o.dma, o.inc, o.sem, o.val, o.pos = e, None, False, False, None, 0, len(self.stream[e])
            o.q = None
            o.deps = []
            o.bar = True
            for li in lasts:
                ol = self.ops[li]
                if not ol.dma and ol.eng != e and ol.fn is not None:
                    ol.inc = True
                    o.deps.append(li)
            self.ops.append(o)
            self.stream[e].append(len(self.ops) - 1)
        self.last_w = {k: v for k, v in self.last_w.items() if k in self.persist}
        self.readers = {k: {} for k in self.last_w}

    def emit(self, nc, stack):
        NS = 12
        MAXC = 20000
        MAXD = 3000
        csem = {}
        ccount = {}
        for e in self.CE:
            csem[e] = stack.enter_context(nc.semaphore("c_" + e))
            ccount[e] = 0
        dsems = {}
        duse = {}
        dcount = {}
        for q in ("sp", "pool", "bg"):
            dsems[q] = [stack.enter_context(nc.semaphore("d_%s_%d" % (q, i))) for i in range(NS)]
            duse[q] = [0] * NS
            dcount[q] = 0
        prevdma = {}
        for idx, o in enumerate(self.ops):
            if o.fn is None:
                continue
            if o.dma:
                q = o.q
                i = dcount[q] % NS
                dcount[q] += 1
                if duse[q][i] >= MAXD:
                    dsems[q][i] = stack.enter_context(nc.semaphore("d_%s_%d_%d" % (q, i, dcount[q])))
                    duse[q][i] = 0
                    prevdma[idx] = None
                else:
                    prevdma[idx] = (dsems[q][i], 16 * duse[q][i]) if duse[q][i] > 0 else None
                duse[q][i] += 1
                o.sem, o.val = dsems[q][i], 16 * duse[q][i]
            elif o.inc:
                e = o.eng
                if ccount[e] >= MAXC:
                    csem[e] = stack.enter_context(nc.semaphore("c_%s_%d" % (e, idx)))
                    ccount[e] = 0
                ccount[e] += 1
                o.sem, o.val = csem[e], ccount[e]
        ops = self.ops
        stream = self.stream
        snap = {}
        cur = {}
        curbg = {}
        for idx, o in enumerate(ops):
            if o.fn is not None and o.dma:
                if o.q == "bg":
                    curbg[id(o.sem)] = (o.sem, o.val)
                else:
                    cur[id(o.sem)] = (o.sem, o.val)
            if o.bar:
                snap[idx] = list(cur.values())
        final = list(cur.values()) + list(curbg.values())

        def run_engine(e, eng):
            waited = {}

            def wait(sem, val):
                k = id(sem)
                if waited.get(k, 0) >= val:
                    return
                waited[k] = val
                eng.wait_ge(sem, val)

            for idx in stream[e]:
                o = ops[idx]
                pend = []

                def want(sem, val):
                    k = id(sem)
                    if waited.get(k, 0) >= val:
                        return
                    waited[k] = val
                    for i_, (s_, v_) in enumerate(pend):
                        if s_ is sem:
                            pend[i_] = (s_, max(v_, val))
                            return
                    pend.append((sem, val))

                for d in o.deps:
                    od = ops[d]
                    want(od.sem, od.val)
                if o.bar:
                    for (s, v) in snap[idx]:
                        want(s, v)
                    for (s_, v_) in pend:
                        eng.wait_ge(s_, v_)
                    continue
                if o.dma and prevdma.get(idx) is not None:
                    want(*prevdma[idx])
                for (s_, v_) in pend[:-1]:
                    eng.wait_ge(s_, v_)
                ins = o.fn(eng)
                if pend:
                    ins._wait_ge(*pend[-1])
                if o.dma:
                    ins.then_inc(o.sem, 16)
                elif o.inc:
                    ins.then_inc(o.sem, 1)
            if e == "sp":
                for (s, v) in final:
                    wait(s, v)
                for ce in self.CE:
                    if ccount[ce] > 0:
                        wait(csem[ce], ccount[ce])

        with nc.Block() as block:
            @block.tensor
            def _(t):
                run_engine("pe", t)

            @block.scalar
            def _(a):
                run_engine("act", a)

            @block.vector
            def _(v):
                run_engine("dve", v)

            @block.gpsimd
            def _(g):
                run_engine("pool", g)

            @block.sync
            def _(s):
                run_engine("sp", s)


class Rot:
    def __init__(self, items):
        self.items = list(items)
        self.i = 0

    def next(self):
        v = self.items[self.i % len(self.items)]
        self.i += 1
        return v


W_SPECS = [
    ("w_in", 2048, 4096), ("wc_kv", 2048, 4096), ("w_out", 2048, 2048), ("wc_q", 2048, 2048),
    ("wc_o", 2048, 2048), ("w_up", 2048, 2 * D_FF), ("w_down", D_FF, 2048),
]


def ffn_tiles(S):
    half = S // 2
    tiles = []
    for hb in (0, half):
        t = 0
        while t < half:
            n = min(510, half - t)
            tiles.append((hb + t, n))
            t += n
    return tiles


def build_program(S, L, dbg=False):
    HALF = S // 2
    NT = S // 512
    NKT = S // 128
    nc = bass.Bass("TRN2", target_bir_lowering=False)
    P = Prog()
    stack = contextlib.ExitStack()

    def din(name, shape):
        return nc.dram_tensor(name, list(shape), F32, kind="ExternalInput").ap()

    def dint(name, shape, dt):
        return nc.dram_tensor(name, list(shape), dt, kind="Internal").ap()

    x_in = din("x", [S, D])
    mem_in = din("mem", [2, N_MEM, D])
    Wd = {}
    for name, K, N in W_SPECS:
        Wd[name] = din(name, [L, K, N])
    g_mix = din("g_mix", [L, D]); g_cross = din("g_cross", [L, D]); g_mem = din("g_mem", [L, D]); g_ffn = din("g_ffn", [L, D])
    g_q = din("g_q", [L, 64]); g_k = din("g_k", [L, 64])
    lam_in = [din(n, [L, 64]) for n in ("lam_q1", "lam_k1", "lam_q2", "lam_k2")]
    g_sub = din("g_sub", [L, 128])
    w_pool = din("w_pool", [L, 4, 256, 256])
    pool_scale = din("pool_scale", [L, 1024])
    gc_q = din("gc_q", [L, 512]); gc_k = din("gc_k", [L, 512])
    conv_w = din("conv_w", [L, 3, 2 * D_FF]); conv_b = din("conv_b", [L, 2 * D_FF])
    cos_in = din("rope_cos", [S, 32]); sin_in = din("rope_sin", [S, 32])
    invcnt_in = din("invcnt", [4, S])
    maskb_in = din("maskb", [128, 2]); bvec_in = din("bvec", [128, 1])
    ident_in = din("ident", [128, 128])
    y_out = nc.dram_tensor("y", [S, D], F32, kind="ExternalOutput").ap()

    Wt = {}
    for name, K, N in W_SPECS:
        kp = (K + 2047) // 2048
        Wt[name] = dint("wt_" + name, [L, kp * (N // 512), 128, 16, 512], BF16)
    xs = dint("xs", [S, D], F32)
    xmid = dint("xmid", [S, D], F32)
    QT = dint("QT", [NH, 128, S], BF16)
    KT = dint("KT", [NH, 128, S], BF16)
    Vd = dint("Vd", [S, 1024], BF16)
    UT = dint("UT", [1024, S + 16], F32)
    AT = dint("AT", [1024, S], BF16)
    HF = dint("HF", [2048, S + 2], BF16)
    CK = dint("CK", [2, 128, 16, 256], BF16)
    CV = dint("CV", [2, 128, 2, 2048], BF16)
    dbg_out = {}

    ARENA = 98000
    arena = stack.enter_context(nc.sbuf_tensor("arena", [128, ARENA], BF16))
    cst = stack.enter_context(nc.sbuf_tensor("cst", [128, 1200], F32))
    identb = stack.enter_context(nc.sbuf_tensor("identb", [128, 128], BF16))
    onesb = stack.enter_context(nc.sbuf_tensor("onesb", [128, 128], BF16))
    pp = [stack.enter_context(nc.psum_tensor("pp%d" % i, [128, 1024], F32)) for i in range(4)]
    psb = []
    for i in range(4):
        psb.append(pp[i][:, 0:512])
        psb.append(pp[i][:, 512:1024])

    class Arena:
        def __init__(self):
            self.off = 0

        def reset(self):
            self.off = 0

        def alloc(self, shape, dt):
            n = int(np.prod(shape[1:]))
            nb = n if dt == BF16 else 2 * n
            nb = (nb + 31) // 32 * 32
            a = arena[0:shape[0], self.off:self.off + nb]
            self.off += nb
            assert self.off <= ARENA, ("arena overflow", self.off)
            if dt == F32:
                a = a.bitcast(F32)
            a = a[:, 0:n]
            if len(shape) == 3:
                a = a.rearrange("p (a b) -> p a b", b=shape[2])
            elif len(shape) == 4:
                a = a.rearrange("p (a b c) -> p a b c", b=shape[2], c=shape[3])
            return a

    AR = Arena()

    c_off = [0]

    def calloc(n):
        a = cst[:, c_off[0]:c_off[0] + n]
        c_off[0] += n
        assert c_off[0] <= 1200
        return a

    maskb = calloc(2); bvec = calloc(1)
    gTall = calloc(128)
    gT = {"mix": gTall[:, 0:16], "cross": gTall[:, 16:32], "mem": gTall[:, 32:48], "ffn": gTall[:, 48:64]}
    gcqT = gTall[:, 64:68]; gckT = gTall[:, 68:72]
    pscale = gTall[:, 72:80]
    gsub_raw = gTall[:, 80:81]
    gq_rep = calloc(64); gk_rep = calloc(64)
    lamt = [calloc(64) for _ in range(4)]
    lamtmp = calloc(64); lamd = calloc(4); nlam = calloc(1); gsubs = calloc(1)
    convw = calloc(3 * 88); convb = calloc(88)
    smalls = calloc(64)
    identf = calloc(128)

    def psbf(b):
        return psb[b][:].bitcast(BF16)

    def dma(out, in_, r=(), w=(), q="sp", bg=False, slow=False):
        if slow:
            return P.op(q, lambda e: e.dma_start(out=out, in_=in_, allow_slow_non_contiguous=True), r=r, w=w, dma=True, bg=bg)
        return P.op(q, lambda e: e.dma_start(out=out, in_=in_), r=r, w=w, dma=True, bg=bg)

    def mm(out, lhsT, rhs, start, stop, r=(), w=()):
        return P.op("pe", lambda e: e.matmul(out, lhsT=lhsT, rhs=rhs, start=start, stop=stop), r=r, w=w)

    def tr(out, in_, r=(), w=()):
        return P.op("pe", lambda e: e.transpose(out, in_, identb[:]), r=r, w=w)

    def act(out, in_, func, r=(), w=(), bias=None, scale=None, accum_out=None):
        kw = {}
        if bias is not None:
            kw["bias"] = bias
        if scale is not None:
            kw["scale"] = scale
        if accum_out is not None:
            kw["accum_out"] = accum_out
        return P.op("act", lambda e: e.activation(out=out, in_=in_, func=func, **kw), r=r, w=w)

    class _Rec:
        def __getattr__(self, name):
            def f(*a, **kw):
                return (name, a, kw)
            return f

    _rec = _Rec()

    def vop(fn, r=(), w=(), eng="dve"):
        name, a, kw = fn(_rec)
        return P.op(eng, lambda e: getattr(e, name)(*a, **kw), r=r, w=w)

    def rstd_ops(dst, src, n, key):
        vop(lambda e: e.tensor_scalar(out=dst, in0=src, scalar1=1.0 / n, scalar2=EPS, op0=ALU.mult, op1=ALU.add), r=[key], w=[key])
        act(dst, dst, AF.Sqrt, r=[key], w=[key])
        vop(lambda e: e.reciprocal(out=dst, in_=dst), r=[key], w=[key])

    dma(cst[:, 0:2], maskb_in, w=["cst"])
    dma(bvec, bvec_in, w=["cst"])
    dma(identf, ident_in, w=["cst"])
    vop(lambda e: e.tensor_copy(out=identb[:], in_=identf), r=["cst"], w=["identb"])
    vop(lambda e: e.memset(onesb[:], 1.0), w=["onesb"])
    zt = AR.alloc([128, 64], F32)
    vop(lambda e: e.memset(zt, 0.0), w=["zt"])
    dma(UT[:, 0:8].rearrange("(c p) t -> p c t", p=128), zt.rearrange("p (c t) -> p c t", t=8), r=["zt"], w=["UTpad"])
    dma(UT[:, S + 8:S + 16].rearrange("(c p) t -> p c t", p=128), zt.rearrange("p (c t) -> p c t", t=8), r=["zt"], w=["UTpad"])
    ztb = AR.alloc([128, 32], BF16)
    vop(lambda e: e.memset(ztb, 0.0), w=["ztb"])
    dma(HF[:, 0:1].rearrange("(c p) t -> p c t", p=128), ztb[:, 0:16].rearrange("p (c t) -> p c t", t=1), r=["ztb"], w=["HFpad"], slow=True)
    dma(HF[:, S + 1:S + 2].rearrange("(c p) t -> p c t", p=128), ztb[:, 0:16].rearrange("p (c t) -> p c t", t=1), r=["ztb"], w=["HFpad"], slow=True)

    def wtiles(name):
        K, N = [(k, n) for (nm, k, n) in W_SPECS if nm == name][0]
        kp = (K + 2047) // 2048
        return kp, N // 512, K

    for l in range(L):
        for name, K, N in W_SPECS:
            kp, nn, _ = wtiles(name)
            for p_ in range(kp):
                k0 = p_ * 2048
                kc = min(16, (K - k0) // 128)
                for n in range(nn):
                    src = Wd[name][l, k0:k0 + kc * 128, n * 512:(n + 1) * 512].rearrange("(c p) j -> p c j", p=128)
                    dma(Wt[name][l, p_ * nn + n, :, 0:kc, :], src, w=[("W", name, l, p_ * nn + n)], q="pool", bg=True)
    P.barrier()

    def load_gT(dst, src_row):
        dma(dst, src_row.rearrange("(c p) -> p c", p=128), w=["gT"])

    def norm_transpose(xt_rows, rkeys, HT, htkey, col0, gTt, HB, hbkeys, ps_pool, nsub, ssq, rows=128):
        for s in range(nsub):
            xa = xt_rows(s)
            hb = HB[s % 2]
            hk = hbkeys[s % 2]
            sq = ssq[:, s:s + 1]
            act(hb, xa, AF.Square, r=[rkeys(s)], w=[hk, ("ssq", s)], accum_out=sq)
            rstd_ops(sq, sq, D, ("ssq", s))
            act(hb, xa, AF.Copy, r=[rkeys(s), ("ssq", s)], w=[hk], scale=sq)
            for half in range(2):
                b = ps_pool.next()
                pv = psbf(b).rearrange("p (c t) -> p c t", t=128)
                for c in range(8):
                    tr(pv[:, c, :], hb[:, (half * 8 + c) * 128:(half * 8 + c + 1) * 128], r=[hk], w=[("ps", b)])
                dst = HT[:, half * 8:half * 8 + 8, col0 + s * 128:col0 + (s + 1) * 128]
                g_b = gTt[:, half * 8:half * 8 + 8].unsqueeze(2).to_broadcast([128, 8, 128])
                vop(lambda e, dst=dst, pv=pv, g_b=g_b: e.tensor_tensor(out=dst, in0=pv, in1=g_b, op=ALU.mult),
                    r=[("ps", b), "gT"], w=[(htkey, s)])

    def load_w(WB, wkeys, name, l, tidx, kc=16):
        b, k = WB.next()
        dma(b[:, 0:kc, :], Wt[name][l, tidx, :, 0:kc, :], r=[("W", name, l, tidx)], w=[k])
        return b, k

    for l in range(L):
        lam_init = 0.8 - 0.6 * math.exp(-0.3 * l)
        xsrc = x_in if l == 0 else xs
        xdst = xs if l < L - 1 else y_out

        AR.reset()
        STG = AR.alloc([128, 128], F32)
        CSTG = AR.alloc([88, 4, 128], F32)
        vop(lambda e: e.memset(STG, 0.0), w=["STG"])
        for r0_, src_ in ((0, g_mix[l]), (16, g_cross[l]), (32, g_mem[l]), (48, g_ffn[l]), (64, gc_q[l]), (68, gc_k[l]), (72, pool_scale[l]), (80, g_sub[l])):
            nr = src_.shape[0] // 128
            dma(STG[r0_:r0_ + nr, :], src_.rearrange("(c p) -> c p", p=128), r=[], w=["STG"])
        P.op("pe", lambda e: e.transpose(psb[0][:, 0:128], STG, identf), r=["STG"], w=[("ps", 0)])
        act(gTall, psb[0][:, 0:128], AF.Copy, r=[("ps", 0)], w=["gT", "gsub", "pscale"])
        for j_ in range(3):
            dma(CSTG[:, j_, :], conv_w[l, j_].rearrange("(c p) -> c p", p=128), w=["CSTG"])
        dma(CSTG[:, 3, :], conv_b[l].rearrange("(c p) -> c p", p=128), w=["CSTG"])
        for j_ in range(4):
            P.op("pe", lambda e, j_=j_: e.transpose(psb[1][:, j_ * 88:(j_ + 1) * 88], CSTG[:, j_, :], identf[0:88, 0:88]), r=["CSTG"], w=[("ps", 1)])
        act(convw, psb[1][:, 0:264], AF.Copy, r=[("ps", 1)], w=["convw"])
        act(convb, psb[1][:, 264:352], AF.Copy, r=[("ps", 1)], w=["convw"])
        dma(gq_rep, g_q[l].partition_broadcast(128), w=["gq"])
        dma(gk_rep, g_k[l].partition_broadcast(128), w=["gq"])
        for i in range(4):
            dma(lamt[i], lam_in[i][l].partition_broadcast(128), w=["lam"])
        for i in range(2):
            vop(lambda e, i=i: e.tensor_tensor(out=lamtmp, in0=lamt[2 * i], in1=lamt[2 * i + 1], op=ALU.mult), r=["lam"], w=["lamtmp"])
            vop(lambda e, i=i: e.tensor_reduce(out=lamd[:, i:i + 1], in_=lamtmp, axis=AX.X, op=ALU.add), r=["lamtmp"], w=["lamd"])
        act(lamd[:, 2:4], lamd[:, 0:2], AF.Exp, r=["lamd"], w=["lamd"])
        vop(lambda e: e.tensor_tensor(out=nlam, in0=lamd[:, 3:4], in1=lamd[:, 2:3], op=ALU.subtract), r=["lamd"], w=["nlam"])
        vop(lambda e: e.tensor_scalar(out=nlam, in0=nlam, scalar1=-lam_init, scalar2=None, op0=ALU.add), r=["nlam"], w=["nlam"])
        vop(lambda e: e.tensor_scalar(out=gsubs, in0=gsub_raw, scalar1=1.0 - lam_init, scalar2=None, op0=ALU.mult), r=["gsub"], w=["gsubs"])
        P.barrier()

        AR.reset()
        WBl = [AR.alloc([128, 16, 512], BF16) for _ in range(3)]
        WB = Rot([(WBl[i], ("wb", i)) for i in range(3)])
        XR = AR.alloc([128, 4, D], F32)
        HB = [AR.alloc([128, D], BF16) for _ in range(2)]
        HT = AR.alloc([128, 16, 512], BF16)
        COS = AR.alloc([128, S // 128, 32], F32)
        SIN = AR.alloc([128, S // 128, 32], F32)
        SQ = [AR.alloc([128, 512], F32) for _ in range(2)]
        QN = [AR.alloc([128, 512], F32) for _ in range(2)]
        RT = [AR.alloc([128, 256], F32) for _ in range(4)]
        QR = [AR.alloc([128, 512], BF16) for _ in range(2)]
        QTs = AR.alloc([128, 16, 512], BF16)
        Vs = AR.alloc([128, 4, 1024], BF16)
        UTs = AR.alloc([128, 8, 512], F32)
        ssq = smalls[:, 0:4]
        qss = [smalls[:, 8:16], smalls[:, 16:24]]
        dma(COS, cos_in.rearrange("(s p) j -> p s j", p=128), w=["cos"])
        dma(SIN, sin_in.rearrange("(s p) j -> p s j", p=128), w=["cos"])
        psG = Rot([0, 1, 2, 3])
        psT = Rot([4, 5])
        psQ = Rot([6, 7])
        evi = 0
        for ti in range(NT):
            t0 = ti * 512
            for s in range(4):
                dma(XR[:, s, :], xsrc[t0 + s * 128:t0 + (s + 1) * 128, :], w=[("XR", s)])
            norm_transpose(lambda s: XR[:, s, :], lambda s: ("XR", s), HT, "HT", 0, gT["mix"], HB, [("HB", 0), ("HB", 1)], psT, 4, ssq)
            htk = [("HT", s) for s in range(4)]
            for n in range(6):
                wb, wk = load_w(WB, None, "w_in", l, n)
                for s in range(4):
                    b = psG.next()
                    for k in range(16):
                        mm(psb[b][:], HT[:, k, s * 128:(s + 1) * 128], wb[:, k, :], k == 0, k == 15, r=[htk[s], wk], w=[("ps", b)])
                    if n < 4:
                        e_ = evi % 2
                        evi += 1
                        sq_, qn_, qr_, qs_ = SQ[e_], QN[e_], QR[e_], qss[e_]
                        kq, kn, kr, ks = ("SQ", e_), ("QN", e_), ("QR", e_), ("qss", e_)
                        grep = gq_rep if n < 2 else gk_rep
                        act(sq_, psb[b][:], AF.Square, r=[("ps", b)], w=[kq])
                        vop(lambda e, sq_=sq_, qs_=qs_: e.tensor_reduce(out=qs_, in_=sq_.rearrange("p (g d) -> p g d", d=64), axis=AX.X, op=ALU.add), r=[kq], w=[ks])
                        rstd_ops(qs_, qs_, 64, ks)
                        pv3 = psb[b][:].rearrange("p (g d) -> p g d", d=64)
                        qn3 = qn_.rearrange("p (g d) -> p g d", d=64)
                        vop(lambda e, qn3=qn3, pv3=pv3, qs_=qs_: e.tensor_tensor(out=qn3, in0=pv3, in1=qs_.unsqueeze(2).to_broadcast([128, 8, 64]), op=ALU.mult),
                            r=[("ps", b), ks], w=[kn])
                        vop(lambda e, qn3=qn3, grep=grep: e.tensor_tensor(out=qn3, in0=qn3, in1=grep.unsqueeze(1).to_broadcast([128, 8, 64]), op=ALU.mult),
                            r=[kn, "gq"], w=[kn])
                        x1 = qn3[:, :, 0:32]
                        x2 = qn3[:, :, 32:64]
                        cs = COS[:, ti * 4 + s, :].unsqueeze(1).to_broadcast([128, 8, 32])
                        sn = SIN[:, ti * 4 + s, :].unsqueeze(1).to_broadcast([128, 8, 32])
                        r3 = [t.rearrange("p (g d) -> p g d", d=32) for t in RT]
                        qr3 = qr_.rearrange("p (g d) -> p g d", d=64)
                        vop(lambda e, x1=x1, cs=cs, r3=r3: e.tensor_tensor(out=r3[0], in0=x1, in1=cs, op=ALU.mult), r=[kn, "cos"], w=[("RT", 0)])
                        vop(lambda e, x2=x2, sn=sn, r3=r3: e.tensor_tensor(out=r3[1], in0=x2, in1=sn, op=ALU.mult), r=[kn, "cos"], w=[("RT", 1)])
                        vop(lambda e, r3=r3, qr3=qr3: e.tensor_tensor(out=qr3[:, :, 0:32], in0=r3[0], in1=r3[1], op=ALU.subtract), r=[("RT", 0), ("RT", 1)], w=[kr])
                        vop(lambda e, x2=x2, cs=cs, r3=r3: e.tensor_tensor(out=r3[2], in0=x2, in1=cs, op=ALU.mult), r=[kn, "cos"], w=[("RT", 2)])
                        vop(lambda e, x1=x1, sn=sn, r3=r3: e.tensor_tensor(out=r3[3], in0=x1, in1=sn, op=ALU.mult), r=[kn, "cos"], w=[("RT", 3)])
                        vop(lambda e, r3=r3, qr3=qr3: e.tensor_tensor(out=qr3[:, :, 32:64], in0=r3[2], in1=r3[3], op=ALU.add), r=[("RT", 2), ("RT", 3)], w=[kr])
                        bq = psQ.next()
                        pq = psbf(bq).rearrange("p (c t) -> p c t", t=128)
                        for hh in range(4):
                            tr(pq[:, hh, :], qr_[:, hh * 128:(hh + 1) * 128], r=[kr], w=[("ps", bq)])
                        act(QTs[:, n * 4:n * 4 + 4, s * 128:(s + 1) * 128], pq[:, 0:4, :], AF.Copy, r=[("ps", bq)], w=[("QTs", n // 2)])
                    else:
                        act(Vs[:, s, (n - 4) * 512:(n - 3) * 512], psb[b][:], AF.Copy, r=[("ps", b)], w=["Vs"])
            dma(QT[:, :, t0:t0 + 512].rearrange("h p t -> p h t"), QTs[:, 0:8, :], r=[("QTs", 0)], w=[("QT", ti)])
            dma(KT[:, :, t0:t0 + 512].rearrange("h p t -> p h t"), QTs[:, 8:16, :], r=[("QTs", 1)], w=[("KT", ti)])
            dma(Vd[t0:t0 + 512, :].rearrange("(s p) n -> p s n", p=128), Vs, r=["Vs"], w=[("Vd", ti)])
            for n in range(6, 8):
                wb, wk = load_w(WB, None, "w_in", l, n)
                for sc in range(4):
                    b = psG.next()
                    for k in range(16):
                        mm(psb[b][:], wb[:, k, sc * 128:(sc + 1) * 128], HT[:, k, :], k == 0, k == 15, r=htk + [wk], w=[("ps", b)])
                    act(UTs[:, (n - 6) * 4 + sc, :], psb[b][:], AF.Copy, r=[("ps", b)], w=["UTs"])
            dma(UT[:, 8 + t0:8 + t0 + 512].rearrange("(c p) t -> p c t", p=128), UTs, r=["UTs"], w=[("UT", ti)])
        P.barrier()

        AR.reset()
        KTb = [AR.alloc([128, S], BF16) for _ in range(2)]
        QTb = [AR.alloc([128, S], BF16) for _ in range(2)]
        Vb = [AR.alloc([128, NKT, 128], BF16) for _ in range(2)]
        Pt = [AR.alloc([128, 1024], BF16) for _ in range(3)]
        Pr = Rot([(Pt[i], ("P", i)) for i in range(3)])
        R0 = AR.alloc([128, 512], F32); R1 = AR.alloc([128, 512], F32)
        T0 = AR.alloc([128, 512], F32); T1 = AR.alloc([128, 512], F32)
        OC = AR.alloc([128, 512], F32); RS = AR.alloc([128, 512], F32)
        SQb = AR.alloc([128, 512], BF16)
        AO = [AR.alloc([128, 512], BF16) for _ in range(2)]
        psS = Rot([0, 1])
        scale = 64 ** -0.5
        it = 0
        for h in range(NH):
            hb_ = h % 2
            dma(KTb[hb_], KT[h], r=[("KT", ti) for ti in range(NT)], w=[("KTb", hb_)])
            dma(QTb[hb_], QT[h], r=[("QT", ti) for ti in range(NT)], w=[("QTb", hb_)])
            dma(Vb[hb_], Vd[:, h * 128:(h + 1) * 128].rearrange("(kt p) d -> p kt d", p=128), r=[("Vd", ti) for ti in range(NT)], w=[("Vb", hb_)])
            kK, kQ, kV = ("KTb", hb_), ("QTb", hb_), ("Vb", hb_)
            for qc in range(NT):
                q0 = qc * 512
                qhalf = 0 if q0 < HALF else 1
                for kt in range(NKT):
                    khalf = 0 if kt * 128 < HALF else 1
                    mb = maskb[:, 0:1] if khalf == qhalf else maskb[:, 1:2]
                    pi = psS.next()
                    for c in range(2):
                        b = 2 * pi + c
                        mm(psb[b][:], KTb[hb_][c * 64:(c + 1) * 64, kt * 128:(kt + 1) * 128], QTb[hb_][c * 64:(c + 1) * 64, q0:q0 + 512], True, True,
                           r=[kK, kQ], w=[("pp", pi)])
                    pt, pk = Pr.next()
                    act(pt, pp[pi][:], AF.Exp, r=[("pp", pi)], w=[pk], bias=mb, scale=scale)
                    for c in range(2):
                        mm(psb[4 + c][:], Vb[hb_][:, kt, :], pt[:, c * 512:(c + 1) * 512], kt == 0, kt == NKT - 1, r=[kV, pk], w=[("ps", 4 + c)])
                        mm(psb[6 + c][:], onesb[:], pt[:, c * 512:(c + 1) * 512], kt == 0, kt == NKT - 1, r=[pk], w=[("ps", 6 + c)])
                vop(lambda e: e.reciprocal(out=R0, in_=psb[6][:]), r=[("ps", 6)], w=["R0"])
                vop(lambda e: e.reciprocal(out=R1, in_=psb[7][:]), r=[("ps", 7)], w=["R1"])
                vop(lambda e: e.tensor_tensor(out=T0, in0=psb[4][:], in1=R0, op=ALU.mult), r=[("ps", 4), "R0"], w=["T0"])
                vop(lambda e: e.tensor_tensor(out=T1, in0=psb[5][:], in1=R1, op=ALU.mult), r=[("ps", 5), "R1"], w=["T1"])
                vop(lambda e: e.scalar_tensor_tensor(out=OC, in0=T1, scalar=nlam, in1=T0, op0=ALU.mult, op1=ALU.add), r=["T0", "T1"], w=["OC"])
                act(SQb, OC, AF.Square, r=["OC"], w=["SQb"])
                mm(psb[6][:], onesb[:], SQb, True, True, r=["SQb"], w=[("ps", 6)])
                vop(lambda e: e.tensor_scalar(out=RS, in0=psb[6][:], scalar1=1.0 / 128, scalar2=EPS, op0=ALU.mult, op1=ALU.add), r=[("ps", 6)], w=["RS"])
                act(RS, RS, AF.Sqrt, r=["RS"], w=["RS"])
                vop(lambda e: e.reciprocal(out=RS, in_=RS), r=["RS"], w=["RS"])
                ao = AO[it % 2]
                ak = ("AO", it % 2)
                it += 1
                vop(lambda e, ao=ao: e.scalar_tensor_tensor(out=ao, in0=OC, scalar=gsubs, in1=RS, op0=ALU.mult, op1=ALU.mult), r=["OC", "RS"], w=[ak])
                dma(AT[h * 128:(h + 1) * 128, q0:q0 + 512], ao, r=[ak], w=[("AT", qc)])
        P.barrier()

        AR.reset()
        WBl = [AR.alloc([128, 16, 512], BF16) for _ in range(3)]
        WB = Rot([(WBl[i], ("wb", i)) for i in range(3)])
        XR = AR.alloc([128, 4, D], F32)
        HB = [AR.alloc([128, D], BF16) for _ in range(2)]
        HTa = AR.alloc([128, 16, 512], BF16)
        HTb = AR.alloc([128, 16, 512], BF16)
        CKs = AR.alloc([128, 16, 256], BF16)
        CVs = AR.alloc([128, 2, 2048], BF16)
        UTt = [AR.alloc([128, 2, 528], F32) for _ in range(2)]
        PW = [AR.alloc([128, 2, 528], F32) for _ in range(2)]
        ZT = [AR.alloc([128, 2, 512], BF16) for _ in range(2)]
        ICN = AR.alloc([128, 4, 512], F32)
        WP = AR.alloc([128, 8, 256], BF16)
        CQN = [AR.alloc([128, 512], BF16) for _ in range(2)]
        CQT = AR.alloc([128, 4, 512], BF16)
        PC = [AR.alloc([128, 512], BF16) for _ in range(4)]
        RC = AR.alloc([128, 512], F32)
        ssq = smalls[:, 0:4]
        cqs = [smalls[:, 8:9], smalls[:, 9:10]]
        psG = Rot([0, 1, 2, 3])
        psT = Rot([4, 5])
        psQ = Rot([6, 7])
        dma(WP, w_pool[l].rearrange("g (cc p) d -> p (g cc) d", p=128), w=["WP"], q="pool")
        for hf in range(2):
            for s in range(2):
                dma(XR[:, s, :], mem_in[hf, s * 128:(s + 1) * 128, :], w=[("XR", s)])
            norm_transpose(lambda s: XR[:, s, :], lambda s: ("XR", s), HTa, "HTa", 0, gT["mem"], HB, [("HB", 0), ("HB", 1)], psT, 2, ssq)
            hk = [("HTa", 0), ("HTa", 1)]
            for n in range(8):
                wb, wk = load_w(WB, None, "wc_kv", l, n)
                for s in range(2):
                    b = psG.next()
                    for k in range(16):
                        mm(psb[b][:], HTa[:, k, s * 128:(s + 1) * 128], wb[:, k, :], k == 0, k == 15, r=[hk[s], wk], w=[("ps", b)])
                    if n < 4:
                        e_ = (n * 2 + s) % 2
                        act(CQN[e_], psb[b][:], AF.Square, r=[("ps", b)], w=[("CQN", e_), ("cqs", e_)], accum_out=cqs[e_])
                        rstd_ops(cqs[e_], cqs[e_], 512, ("cqs", e_))
                        act(CQN[e_], psb[b][:], AF.Copy, r=[("ps", b), ("cqs", e_)], w=[("CQN", e_)], scale=cqs[e_])
                        bq = psQ.next()
                        pq = psbf(bq).rearrange("p (c t) -> p c t", t=128)
                        for dc in range(4):
                            tr(pq[:, dc, :], CQN[e_][:, dc * 128:(dc + 1) * 128], r=[("CQN", e_)], w=[("ps", bq)])
                        dstk = CKs[:, n * 4:n * 4 + 4, s * 128:(s + 1) * 128]
                        g_b = gckT[:, 0:4].unsqueeze(2).to_broadcast([128, 4, 128])
                        vop(lambda e, dstk=dstk, pq=pq, g_b=g_b: e.tensor_tensor(out=dstk, in0=pq[:, 0:4, :], in1=g_b, op=ALU.mult), r=[("ps", bq), "gT"], w=["CKs"])
                    else:
                        act(CVs[:, s, (n - 4) * 512:(n - 3) * 512], psb[b][:], AF.Copy, r=[("ps", b)], w=["CVs"])
            dma(CK[hf], CKs, r=["CKs"], w=[("CK", hf)])
            dma(CV[hf], CVs, r=["CVs"], w=[("CV", hf)])
        for ti in range(NT):
            t0 = ti * 512
            hf = 0 if t0 < HALF else 1
            for s in range(4):
                dma(XR[:, s, :], xsrc[t0 + s * 128:t0 + (s + 1) * 128, :], w=[("XR", s)])
            dma(CKs, CK[hf], r=[("CK", hf)], w=["CKs"])
            dma(CVs, CV[hf], r=[("CV", hf)], w=["CVs"])
            dma(ICN, invcnt_in[:, t0:t0 + 512].partition_broadcast(128), w=["ICN"])
            dma(HTa[:, 0:8, :], AT[:, t0:t0 + 512].rearrange("(c p) t -> p c t", p=128), r=[("AT", ti)], w=[("HTa", c) for c in range(8)])
            for g in range(4):
                wdw = 2 << g
                u = UTt[g % 2]
                uk = ("UTt", g % 2)
                dma(u, UT[g * 256:(g + 1) * 256, t0:t0 + 528].rearrange("(c p) t -> p c t", p=128),
                    r=[("UT", tj) for tj in range(max(0, ti - 1), min(NT, ti + 2))] + ["UTpad"], w=[uk])
                if t0 + 512 == HALF:
                    vop(lambda e, u=u: e.tensor_scalar(out=u[:, :, 520:528], in0=u[:, :, 520:528], scalar1=bvec, scalar2=None, op0=ALU.mult), r=[uk], w=[uk])
                if t0 == HALF:
                    vop(lambda e, u=u: e.tensor_scalar(out=u[:, :, 0:8], in0=u[:, :, 0:8], scalar1=bvec, scalar2=None, op0=ALU.mult), r=[uk], w=[uk])
                a_, b_ = PW[0], PW[1]
                vop(lambda e, u=u, a_=a_: e.tensor_tensor(out=a_[:, :, 1:528], in0=u[:, :, 0:527], in1=u[:, :, 1:528], op=ALU.add), r=[uk], w=["PW0"])
                cur, curk, oth, othk = a_, "PW0", b_, "PW1"
                lo, hi = 1, 528
                sh = 1
                for step in range(g):
                    nlo, nhi = lo + sh, hi - sh
                    vop(lambda e, cur=cur, oth=oth, nlo=nlo, nhi=nhi, sh=sh: e.tensor_tensor(out=oth[:, :, nlo:nhi], in0=cur[:, :, nlo - sh:nhi - sh], in1=cur[:, :, nlo + sh:nhi + sh], op=ALU.add),
                        r=[curk], w=[othk])
                    cur, curk, oth, othk = oth, othk, cur, curk
                    lo, hi = nlo, nhi
                    sh *= 2
                assert lo <= 8 and hi >= 520
                icb = ICN[:, g, :].unsqueeze(1).to_broadcast([128, 2, 512])
                vop(lambda e, cur=cur, oth=oth, icb=icb: e.tensor_tensor(out=oth[:, :, 8:520], in0=cur[:, :, 8:520], in1=icb, op=ALU.mult), r=[curk, "ICN"], w=[othk])
                z = ZT[g % 2]
                zk = ("ZT", g % 2)
                vop(lambda e, oth=oth, u=u, z=z: e.tensor_tensor(out=z, in0=oth[:, :, 8:520], in1=u[:, :, 8:520], op=ALU.subtract), r=[othk, uk], w=[zk])
                for dd in range(2):
                    b = psQ.next()
                    for cc in range(2):
                        mm(psb[b][:], WP[:, g * 2 + cc, dd * 128:(dd + 1) * 128], z[:, cc, :], cc == 0, cc == 1, r=[zk, "WP"], w=[("ps", b)])
                    act(HTa[:, 8 + 2 * g + dd, :], psb[b][:], AF.Copy, r=[("ps", b), "pscale"], w=[("HTa", 8 + 2 * g + dd)], scale=pscale[:, 2 * g + dd:2 * g + dd + 1])
            mk = [("HTa", i) for i in range(16)]
            for n in range(4):
                wb, wk = load_w(WB, None, "w_out", l, n)
                for s in range(4):
                    b = psG.next()
                    for k in range(16):
                        mm(psb[b][:], HTa[:, k, s * 128:(s + 1) * 128], wb[:, k, :], k == 0, k == 15, r=mk + [wk], w=[("ps", b)])
                    xa = XR[:, s, n * 512:(n + 1) * 512]
                    vop(lambda e, xa=xa, b=b: e.tensor_tensor(out=xa, in0=psb[b][:], in1=xa, op=ALU.add), r=[("ps", b), ("XR", s)], w=[("XR", s)])
            norm_transpose(lambda s: XR[:, s, :], lambda s: ("XR", s), HTb, "HTb", 0, gT["cross"], HB, [("HB", 0), ("HB", 1)], psT, 4, ssq)
            hbk = [("HTb", s) for s in range(4)]
            ei = 0
            for hh in range(4):
                wb, wk = load_w(WB, None, "wc_q", l, hh)
                for s in range(4):
                    b = psG.next()
                    for k in range(16):
                        mm(psb[b][:], HTb[:, k, s * 128:(s + 1) * 128], wb[:, k, :], k == 0, k == 15, r=[hbk[s], wk], w=[("ps", b)])
                    e_ = ei % 2
                    ei += 1
                    act(CQN[e_], psb[b][:], AF.Square, r=[("ps", b)], w=[("CQN", e_), ("cqs", e_)], accum_out=cqs[e_])
                    rstd_ops(cqs[e_], cqs[e_], 512, ("cqs", e_))
                    act(CQN[e_], psb[b][:], AF.Copy, r=[("ps", b), ("cqs", e_)], w=[("CQN", e_)], scale=cqs[e_])
                    bq = psQ.next()
                    pq = psbf(bq).rearrange("p (c t) -> p c t", t=128)
                    for dc in range(4):
                        tr(pq[:, dc, :], CQN[e_][:, dc * 128:(dc + 1) * 128], r=[("CQN", e_)], w=[("ps", bq)])
                    dstq = CQT[:, 0:4, s * 128:(s + 1) * 128]
                    g_b = gcqT[:, 0:4].unsqueeze(2).to_broadcast([128, 4, 128])
                    vop(lambda e, dstq=dstq, pq=pq, g_b=g_b: e.tensor_tensor(out=dstq, in0=pq[:, 0:4, :], in1=g_b, op=ALU.mult), r=[("ps", bq), "gT"], w=["CQT"])
                pcs = []
                for mc in range(2):
                    b = psG.next()
                    for dc in range(4):
                        mm(psb[b][:], CKs[:, hh * 4 + dc, mc * 128:(mc + 1) * 128], CQT[:, dc, :], dc == 0, dc == 3, r=["CKs", "CQT"], w=[("ps", b)])
                    pc = PC[(hh * 2 + mc) % 4]
                    pk = ("PC", (hh * 2 + mc) % 4)
                    act(pc, psb[b][:], AF.Exp, r=[("ps", b)], w=[pk], scale=512 ** -0.5)
                    pcs.append((pc, pk))
                bz = psG.next()
                for mc in range(2):
                    mm(psb[bz][:], onesb[:], pcs[mc][0], mc == 0, mc == 1, r=[pcs[mc][1]], w=[("ps", bz)])
                vop(lambda e, bz=bz: e.reciprocal(out=RC, in_=psb[bz][:]), r=[("ps", bz)], w=["RC"])
                for dvc in range(4):
                    b = psG.next()
                    for mc in range(2):
                        mm(psb[b][:], CVs[:, mc, hh * 512 + dvc * 128:hh * 512 + (dvc + 1) * 128], pcs[mc][0], mc == 0, mc == 1, r=["CVs", pcs[mc][1]], w=[("ps", b)])
                    dsto = HTa[:, hh * 4 + dvc, :]
                    vop(lambda e, dsto=dsto, b=b: e.tensor_tensor(out=dsto, in0=psb[b][:], in1=RC, op=ALU.mult), r=[("ps", b), "RC"], w=[("HTa", hh * 4 + dvc)])
            ok = mk
            for n in range(4):
                wb, wk = load_w(WB, None, "wc_o", l, n)
                for s in range(4):
                    b = psG.next()
                    for k in range(16):
                        mm(psb[b][:], HTa[:, k, s * 128:(s + 1) * 128], wb[:, k, :], k == 0, k == 15, r=ok + [wk], w=[("ps", b)])
                    xa = XR[:, s, n * 512:(n + 1) * 512]
                    vop(lambda e, xa=xa, b=b: e.tensor_tensor(out=xa, in0=psb[b][:], in1=xa, op=ALU.add), r=[("ps", b), ("XR", s)], w=[("XR", s)])
            for s in range(4):
                dma(xmid[t0 + s * 128:t0 + (s + 1) * 128, :], XR[:, s, :], r=[("XR", s)], w=[("xmid", ti)])
            norm_transpose(lambda s: XR[:, s, :], lambda s: ("XR", s), HTb, "HTb", 0, gT["ffn"], HB, [("HB", 0), ("HB", 1)], psT, 4, ssq)
            dma(HF[:, 1 + t0:1 + t0 + 512].rearrange("(c p) t -> p c t", p=128), HTb, r=hbk, w=[("HF", ti)])
        P.barrier()

        AR.reset()
        WBl = [AR.alloc([128, 16, 512], BF16) for _ in range(4)]
        WB = Rot([(WBl[i], ("wb", i)) for i in range(4)])
        HTf = [AR.alloc([128, 16, 512], BF16) for _ in range(2)]
        ACTT = AR.alloc([128, 44, 512], BF16)
        XF = [AR.alloc([128, D], F32) for _ in range(4)]
        CA = [AR.alloc([128, 512], F32) for _ in range(4)]
        GL = [AR.alloc([128, 512], F32) for _ in range(2)]
        psU = Rot([0, 1, 2, 3])
        psD = [4, 5, 6, 7]
        hfk_all = [("HF", tj) for tj in range(NT)] + ["HFpad"]
        xi = 0
        for fi, (ts, tn) in enumerate(ffn_tiles(S)):
            wcols = tn + 2
            ht = HTf[fi % 2]
            hk_ = ("HTf", fi % 2)
            dma(ht[:, :, 0:wcols], HF[:, ts:ts + wcols].rearrange("(c p) t -> p c t", p=128), r=hfk_all, w=[hk_])
            if ts + tn == HALF:
                vop(lambda e, ht=ht, wcols=wcols: e.tensor_scalar(out=ht[:, :, wcols - 1:wcols], in0=ht[:, :, wcols - 1:wcols], scalar1=bvec, scalar2=None, op0=ALU.mult), r=[hk_], w=[hk_])
            if ts == HALF:
                vop(lambda e, ht=ht: e.tensor_scalar(out=ht[:, :, 0:1], in0=ht[:, :, 0:1], scalar1=bvec, scalar2=None, op0=ALU.mult), r=[hk_], w=[hk_])
            for i in range(11):
                wg, wgk = load_w(WB, None, "w_up", l, i)
                wv, wvk = load_w(WB, None, "w_up", l, 11 + i)
                for sc in range(4):
                    j = i * 4 + sc
                    res = []
                    for which, (wb, wk) in enumerate(((wg, wgk), (wv, wvk))):
                        b = psU.next()
                        for k in range(16):
                            mm(psb[b][:, 0:wcols], wb[:, k, sc * 128:(sc + 1) * 128], ht[:, k, 0:wcols], k == 0, k == 15, r=[hk_, wk], w=[("ps", b)])
                        ch = j if which == 0 else 44 + j
                        ca = CA[which * 2 + (j % 2)]
                        cak = ("CA", which * 2 + (j % 2))
                        act(ca[:, 0:tn], psb[b][:, 1:1 + tn], AF.Identity, r=[("ps", b), "convw"], w=[cak],
                            bias=convb[:, ch:ch + 1], scale=convw[:, 88 + ch:88 + ch + 1])
                        vop(lambda e, ca=ca, b=b, ch=ch: e.scalar_tensor_tensor(out=ca[:, 0:tn], in0=psb[b][:, 0:tn], scalar=convw[:, ch:ch + 1], in1=ca[:, 0:tn], op0=ALU.mult, op1=ALU.add),
                            r=[("ps", b), cak], w=[cak])
                        vop(lambda e, ca=ca, b=b, ch=ch: e.scalar_tensor_tensor(out=ca[:, 0:tn], in0=psb[b][:, 2:2 + tn], scalar=convw[:, 176 + ch:176 + ch + 1], in1=ca[:, 0:tn], op0=ALU.mult, op1=ALU.add),
                            r=[("ps", b), cak], w=[cak])
                        res.append((ca, cak))
                    gl = GL[j % 2]
                    glk = ("GL", j % 2)
                    act(gl[:, 0:tn], res[0][0][:, 0:tn], AF.Gelu_apprx_tanh, r=[res[0][1]], w=[glk])
                    vop(lambda e, gl=gl, cv_=res[1][0], j=j: e.tensor_tensor(out=ACTT[:, j, 0:tn], in0=gl[:, 0:tn], in1=cv_[:, 0:tn], op=ALU.mult),
                        r=[glk, res[1][1]], w=[("ACTT", j)], eng="pool")
            ak_all = [("ACTT", j) for j in range(44)]
            subs = []
            o = 0
            while o < tn:
                subs.append((o, min(128, tn - o)))
                o += 128
            for n in range(4):
                for p_ in range(3):
                    kc = 16 if p_ < 2 else 12
                    wb, wk = load_w(WB, None, "w_down", l, p_ * 4 + n, kc=kc)
                    for si, (so, sn_) in enumerate(subs):
                        b = psD[si]
                        for k in range(kc):
                            kk = p_ * 16 + k
                            mm(psb[b][0:sn_, :], ACTT[:, kk, so:so + sn_], wb[:, k, :], kk == 0, kk == 43, r=ak_all + [wk], w=[("ps", b)])
                for si, (so, sn_) in enumerate(subs):
                    b = psD[si]
                    if n == 0:
                        pass
                    xf = XF[si]
                    xfk = ("XF", si)
                    if n == 0:
                        dma(xf[0:sn_, :], xmid[ts + so:ts + so + sn_, :], r=[("xmid", tj) for tj in range(NT)], w=[xfk])
                    xa = xf[0:sn_, n * 512:(n + 1) * 512]
                    vop(lambda e, xa=xa, b=b, sn_=sn_: e.tensor_tensor(out=xa, in0=psb[b][0:sn_, :], in1=xa, op=ALU.add), r=[("ps", b), xfk], w=[xfk])
                    if n == 3:
                        dma(xdst[ts + so:ts + so + sn_, :], xf[0:sn_, :], r=[xfk], w=[("xout", fi, si)])
            xi += len(subs)
        P.barrier()

    P.emit(nc, stack)
    stack.close()
    return nc


def core_tables(S, split):
    half = S // 2
    pos = np.arange(S)
    if split:
        pos = pos % half
        seqlen = half
    else:
        seqlen = S
    inv = (10000.0 ** (-np.arange(32, dtype=np.float32) / 32)).astype(np.float32)
    ang = pos.astype(np.float32)[:, None] * inv[None, :]
    cos = np.cos(ang).astype(np.float32)
    sin = np.sin(ang).astype(np.float32)
    invcnt = np.zeros((4, S), np.float32)
    for g, w in enumerate((2, 4, 8, 16)):
        lo = np.clip(pos - w // 2, 0, seqlen)
        hi = np.clip(pos + w // 2, 0, seqlen)
        invcnt[g] = 1.0 / (hi - lo).astype(np.float32)
    maskb = np.zeros((128, 2), np.float32)
    maskb[:, 1] = NEG if split else 0.0
    bvec = np.full((128, 1), 0.0 if split else 1.0, np.float32)
    return cos, sin, invcnt, maskb, bvec


_NC_CACHE = {}


def run_cores(core_specs, weights, S, L):
    key = (S, L)
    if key not in _NC_CACHE:
        _NC_CACHE[key] = build_program(S, L)
    nc = _NC_CACHE[key]
    ident = np.eye(128, dtype=np.float32)
    in_maps = []
    for x, mem, split in core_specs:
        cos, sin, invcnt, maskb, bvec = core_tables(S, split)
        m = {"x": np.ascontiguousarray(x, dtype=np.float32), "mem": np.ascontiguousarray(mem, dtype=np.float32),
             "rope_cos": cos, "rope_sin": sin, "invcnt": invcnt, "maskb": maskb, "bvec": bvec, "ident": ident}
        for k, v in weights.items():
            m[k] = np.ascontiguousarray(v, dtype=np.float32)
        in_maps.append(m)
    res = run_bass_kernel_spmd(nc, in_maps, core_ids=list(range(len(core_specs))))
    return [r["y"] for r in res.results]


WEIGHT_NAMES = ["g_mix", "w_in", "g_q", "g_k", "lam_q1", "lam_k1", "lam_q2", "lam_k2", "g_sub", "w_pool", "pool_scale",
                "w_out", "g_cross", "g_mem", "wc_q", "wc_kv", "gc_q", "gc_k", "wc_o", "g_ffn", "w_up", "conv_w", "conv_b", "w_down"]


def kernel(**inputs):
    xp = np.asarray(inputs["x_prompt"]); xsm = np.asarray(inputs["x_sample"])
    mp = np.asarray(inputs["mem_prompt"]); ms = np.asarray(inputs["mem_sample"])
    weights = {k: np.asarray(inputs[k]) for k in WEIGHT_NAMES}
    L = weights["w_in"].shape[0]
    S = xp.shape[1]
    specs = []
    for i in range(4):
        specs.append((xp[i], np.stack([mp[i], mp[i]]), False))
    for j in range(4):
        specs.append((xsm[2 * j:2 * j + 2].reshape(S, D), ms[2 * j:2 * j + 2], True))
    ys = run_cores(specs, weights, S, L)
    y_prompt = np.stack(ys[0:4]).astype(np.float32)
    y_sample = np.concatenate([y.reshape(2, S // 2, D) for y in ys[4:8]], axis=0).astype(np.float32)
    return (y_prompt, y_sample)
```

```python
import contextlib
import math
import numpy as np
import concourse.bass as bass
import concourse.mybir as mybir
from concourse.bass_utils import run_bass_kernel_spmd

F32 = mybir.dt.float32
BF16 = mybir.dt.bfloat16
AF = mybir.ActivationFunctionType
ALU = mybir.AluOpType
AX = mybir.AxisListType

D = 2048
NH = 8
N_MEM = 256
D_FF = 5632
EPS = 1e-6
NEG = -30000.0


class Op:
    __slots__ = ("eng", "fn", "dma", "deps", "inc", "sem", "val", "bar", "pos", "q")


class Prog:
    CE = ("pe", "act", "dve", "pool")
    ALL = ("pe", "act", "dve", "pool", "sp")

    def __init__(self):
        self.ops = []
        self.last_w = {}
        self.readers = {}
        self.stream = {e: [] for e in self.ALL}
        self.persist = set()

    def op(self, eng, fn, r=(), w=(), dma=False, bg=False):
        o = Op()
        o.eng, o.fn, o.dma, o.inc, o.sem, o.val, o.bar = eng, fn, dma, False, None, 0, None
        o.q = ("bg" if bg else eng) if dma else None
        idx = len(self.ops)
        deps = set()
        for k in r:
            lw = self.last_w.get(k)
            if lw is not None:
                deps.add(lw)
        for k in w:
            lw = self.last_w.get(k)
            if lw is not None:
                deps.add(lw)
            rd = self.readers.get(k)
            if rd:
                deps.update(rd.values())
        for k in r:
            rd = self.readers.setdefault(k, {})
            rd[("d", idx) if dma else eng] = idx
        for k in w:
            self.last_w[k] = idx
            self.readers[k] = {}
            if bg:
                self.persist.add(k)
        best = {}
        keep = []
        for d in deps:
            od = self.ops[d]
            if od.dma:
                keep.append(d)
            else:
                if od.eng == eng and eng == "pe":
                    continue
                if od.eng not in best or best[od.eng] < d:
                    best[od.eng] = d
        keep.extend(best.values())
        o.deps = keep
        for d in keep:
            self.ops[d].inc = True
        o.pos = len(self.stream[eng])
        self.ops.append(o)
        self.stream[eng].append(idx)
        return idx

    def barrier(self):
        lasts = []
        for e in self.ALL:
            for li in reversed(self.stream[e]):
                ol = self.ops[li]
                if ol.bar:
                    break
                if not ol.dma:
                    lasts.append(li)
                    break
        for e in self.ALL:
            o = Op()
            o.eng, o.fn, o.dma, o.inc, o.sem, o.val, o.pos = e, None, False, False, None, 0, len(self.stream[e])
            o.q = None
            o.deps = []
            o.bar = True
            for li in lasts:
                ol = self.ops[li]
                if not ol.dma and ol.eng != e and ol.fn is not None:
                    ol.inc = True
                    o.deps.append(li)
            self.ops.append(o)
            self.stream[e].append(len(self.ops) - 1)
        self.last_w = {k: v for k, v in self.last_w.items() if k in self.persist}
        self.readers = {k: {} for k in self.last_w}

    def emit(self, nc, stack):
        NS = 12
        MAXC = 20000
        MAXD = 3000
        csem = {}
        ccount = {}
        for e in self.CE:
            csem[e] = stack.enter_context(nc.semaphore("c_" + e))
            ccount[e] = 0
        dsems = {}
        duse = {}
        dcount = {}
        for q in ("sp", "pool", "bg"):
            dsems[q] = [stack.enter_context(nc.semaphore("d_%s_%d" % (q, i))) for i in range(NS)]
            duse[q] = [0] * NS
            dcount[q] = 0
        prevdma = {}
        for idx, o in enumerate(self.ops):
            if o.fn is None:
                continue
            if o.dma:
                q = o.q
                i = dcount[q] % NS
                dcount[q] += 1
                if duse[q][i] >= MAXD:
                    dsems[q][i] = stack.enter_context(nc.semaphore("d_%s_%d_%d" % (q, i, dcount[q])))
                    duse[q][i] = 0
                    prevdma[idx] = None
                else:
                    prevdma[idx] = (dsems[q][i], 16 * duse[q][i]) if duse[q][i] > 0 else None
                duse[q][i] += 1
                o.sem, o.val = dsems[q][i], 16 * duse[q][i]
            elif o.inc:
                e = o.eng
                if ccount[e] >= MAXC:
                    csem[e] = stack.enter_context(nc.semaphore("c_%s_%d" % (e, idx)))
                    ccount[e] = 0
                ccount[e] += 1
                o.sem, o.val = csem[e], ccount[e]
        ops = self.ops
        stream = self.stream
        snap = {}
        cur = {}
        curbg = {}
        for idx, o in enumerate(ops):
            if o.fn is not None and o.dma:
                if o.q == "bg":
                    curbg[id(o.sem)] = (o.sem, o.val)
                else:
                    cur[id(o.sem)] = (o.sem, o.val)
            if o.bar:
                snap[idx] = list(cur.values())
        final = list(cur.values()) + list(curbg.values())

        def run_engine(e, eng):
            waited = {}

            def wait(sem, val):
                k = id(sem)
                if waited.get(k, 0) >= val:
                    return
                waited[k] = val
                eng.wait_ge(sem, val)

            for idx in stream[e]:
                o = ops[idx]
                pend = []

                def want(sem, val):
                    k = id(sem)
                    if waited.get(k, 0) >= val:
                        return
                    waited[k] = val
                    for i_, (s_, v_) in enumerate(pend):
                        if s_ is sem:
                            pend[i_] = (s_, max(v_, val))
                            return
                    pend.append((sem, val))

                for d in o.deps:
                    od = ops[d]
                    want(od.sem, od.val)
                if o.bar:
                    for (s, v) in snap[idx]:
                        want(s, v)
                    for (s_, v_) in pend:
                        eng.wait_ge(s_, v_)
                    continue
                if o.dma and prevdma.get(idx) is not None:
                    want(*prevdma[idx])
                for (s_, v_) in pend[:-1]:
                    eng.wait_ge(s_, v_)
                ins = o.fn(eng)
                if pend:
                    ins._wait_ge(*pend[-1])
                if o.dma:
                    ins.then_inc(o.sem, 16)
                elif o.inc:
                    ins.then_inc(o.sem, 1)
            if e == "sp":
                for (s, v) in final:
                    wait(s, v)
                for ce in self.CE:
                    if ccount[ce] > 0:
                        wait(csem[ce], ccount[ce])

        with nc.Block() as block:
            @block.tensor
            def _(t):
                run_engine("pe", t)

            @block.scalar
            def _(a):
                run_engine("act", a)

            @block.vector
            def _(v):
                run_engine("dve", v)

            @block.gpsimd
            def _(g):
                run_engine("pool", g)

            @block.sync
            def _(s):
                run_engine("sp", s)


class Rot:
    def __init__(self, items):
        self.items = list(items)
        self.i = 0

    def next(self):
        v = self.items[self.i % len(self.items)]
        self.i += 1
        return v


W_SPECS = [
    ("w_in", 2048, 4096), ("wc_kv", 2048, 4096), ("w_out", 2048, 2048), ("wc_q", 2048, 2048),
    ("wc_o", 2048, 2048), ("w_up", 2048, 2 * D_FF), ("w_down", D_FF, 2048),
]


def ffn_tiles(S):
    half = S // 2
    tiles = []
    for hb in (0, half):
        t = 0
        while t < half:
            n = min(510, half - t)
            tiles.append((hb + t, n))
            t += n
    return tiles


def build_program(S, L, dbg=False):
    HALF = S // 2
    NT = S // 512
    NKT = S // 128
    nc = bass.Bass("TRN2", target_bir_lowering=False)
    P = Prog()
    stack = contextlib.ExitStack()

    def din(name, shape):
        return nc.dram_tensor(name, list(shape), F32, kind="ExternalInput").ap()

    def dint(name, shape, dt):
        return nc.dram_tensor(name, list(shape), dt, kind="Internal").ap()

    x_in = din("x", [S, D])
    mem_in = din("mem", [2, N_MEM, D])
    Wd = {}
    for name, K, N in W_SPECS:
        Wd[name] = din(name, [L, K, N])
    g_mix = din("g_mix", [L, D]); g_cross = din("g_cross", [L, D]); g_mem = din("g_mem", [L, D]); g_ffn = din("g_ffn", [L, D])
    g_q = din("g_q", [L, 64]); g_k = din("g_k", [L, 64])
    lam_in = [din(n, [L, 64]) for n in ("lam_q1", "lam_k1", "lam_q2", "lam_k2")]
    g_sub = din("g_sub", [L, 128])
    w_pool = din("w_pool", [L, 4, 256, 256])
    pool_scale = din("pool_scale", [L, 1024])
    gc_q = din("gc_q", [L, 512]); gc_k = din("gc_k", [L, 512])
    conv_w = din("conv_w", [L, 3, 2 * D_FF]); conv_b = din("conv_b", [L, 2 * D_FF])
    cos_in = din("rope_cos", [S, 32]); sin_in = din("rope_sin", [S, 32])
    invcnt_in = din("invcnt", [4, S])
    maskb_in = din("maskb", [128, 2]); bvec_in = din("bvec", [128, 1])
    ident_in = din("ident", [128, 128])
    y_out = nc.dram_tensor("y", [S, D], F32, kind="ExternalOutput").ap()

    Wt = {}
    for name, K, N in W_SPECS:
        kp = (K + 2047) // 2048
        Wt[name] = dint("wt_" + name, [L, kp * (N // 512), 128, 16, 512], BF16)
    xs = dint("xs", [S, D], F32)
    xmid = dint("xmid", [S, D], F32)
    QT = dint("QT", [NH, 128, S], BF16)
    KT = dint("KT", [NH, 128, S], BF16)
    Vd = dint("Vd", [S, 1024], BF16)
    UT = dint("UT", [1024, S + 16], F32)
    AT = dint("AT", [1024, S], BF16)
    HF = dint("HF", [2048, S + 2], BF16)
    CK = dint("CK", [2, 128, 16, 256], BF16)
    CV = dint("CV", [2, 128, 2, 2048], BF16)
    dbg_out = {}

    ARENA = 98000
    arena = stack.enter_context(nc.sbuf_tensor("arena", [128, ARENA], BF16))
    cst = stack.enter_context(nc.sbuf_tensor("cst", [128, 1200], F32))
    identb = stack.enter_context(nc.sbuf_tensor("identb", [128, 128], BF16))
    onesb = stack.enter_context(nc.sbuf_tensor("onesb", [128, 128], BF16))
    pp = [stack.enter_context(nc.psum_tensor("pp%d" % i, [128, 1024], F32)) for i in range(4)]
    psb = []
    for i in range(4):
        psb.append(pp[i][:, 0:512])
        psb.append(pp[i][:, 512:1024])

    class Arena:
        def __init__(self):
            self.off = 0

        def reset(self):
            self.off = 0

        def alloc(self, shape, dt):
            n = int(np.prod(shape[1:]))
            nb = n if dt == BF16 else 2 * n
            nb = (nb + 31) // 32 * 32
            a = arena[0:shape[0], self.off:self.off + nb]
            self.off += nb
            assert self.off <= ARENA, ("arena overflow", self.off)
            if dt == F32:
                a = a.bitcast(F32)
            a = a[:, 0:n]
            if len(shape) == 3:
                a = a.rearrange("p (a b) -> p a b", b=shape[2])
            elif len(shape) == 4:
                a = a.rearrange("p (a b c) -> p a b c", b=shape[2], c=shape[3])
            return a

    AR = Arena()

    c_off = [0]

    def calloc(n):
        a = cst[:, c_off[0]:c_off[0] + n]
        c_off[0] += n
        assert c_off[0] <= 1200
        return a

    maskb = calloc(2); bvec = calloc(1)
    gTall = calloc(128)
    gT = {"mix": gTall[:, 0:16], "cross": gTall[:, 16:32], "mem": gTall[:, 32:48], "ffn": gTall[:, 48:64]}
    gcqT = gTall[:, 64:68]; gckT = gTall[:, 68:72]
    pscale = gTall[:, 72:80]
    gsub_raw = gTall[:, 80:81]
    gq_rep = calloc(64); gk_rep = calloc(64)
    lamt = [calloc(64) for _ in range(4)]
    lamtmp = calloc(64); lamd = calloc(4); nlam = calloc(1); gsubs = calloc(1)
    convw = calloc(3 * 88); convb = calloc(88)
    smalls = calloc(64)
    identf = calloc(128)

    def psbf(b):
        return psb[b][:].bitcast(BF16)

    def dma(out, in_, r=(), w=(), q="sp", bg=False, slow=False):
        if slow:
            return P.op(q, lambda e: e.dma_start(out=out, in_=in_, allow_slow_non_contiguous=True), r=r, w=w, dma=True, bg=bg)
        return P.op(q, lambda e: e.dma_start(out=out, in_=in_), r=r, w=w, dma=True, bg=bg)

    def mm(out, lhsT, rhs, start, stop, r=(), w=()):
        return P.op("pe", lambda e: e.matmul(out, lhsT=lhsT, rhs=rhs, start=start, stop=stop), r=r, w=w)

    def tr(out, in_, r=(), w=()):
        return P.op("pe", lambda e: e.transpose(out, in_, identb[:]), r=r, w=w)

    def act(out, in_, func, r=(), w=(), bias=None, scale=None, accum_out=None):
        kw = {}
        if bias is not None:
            kw["bias"] = bias
        if scale is not None:
            kw["scale"] = scale
        if accum_out is not None:
            kw["accum_out"] = accum_out
        return P.op("act", lambda e: e.activation(out=out, in_=in_, func=func, **kw), r=r, w=w)

    class _Rec:
        def __getattr__(self, name):
            def f(*a, **kw):
                return (name, a, kw)
            return f

    _rec = _Rec()

    def vop(fn, r=(), w=(), eng="dve"):
        name, a, kw = fn(_rec)
        return P.op(eng, lambda e: getattr(e, name)(*a, **kw), r=r, w=w)

    def rstd_ops(dst, src, n, key):
        vop(lambda e: e.tensor_scalar(out=dst, in0=src, scalar1=1.0 / n, scalar2=EPS, op0=ALU.mult, op1=ALU.add), r=[key], w=[key])
        act(dst, dst, AF.Sqrt, r=[key], w=[key])
        vop(lambda e: e.reciprocal(out=dst, in_=dst), r=[key], w=[key])

    dma(cst[:, 0:2], maskb_in, w=["cst"])
    dma(bvec, bvec_in, w=["cst"])
    dma(identf, ident_in, w=["cst"])
    vop(lambda e: e.tensor_copy(out=identb[:], in_=identf), r=["cst"], w=["identb"])
    vop(lambda e: e.memset(onesb[:], 1.0), w=["onesb"])
    zt = AR.alloc([128, 64], F32)
    vop(lambda e: e.memset(zt, 0.0), w=["zt"])
    dma(UT[:, 0:8].rearrange("(c p) t -> p c t", p=128), zt.rearrange("p (c t) -> p c t", t=8), r=["zt"], w=["UTpad"])
    dma(UT[:, S + 8:S + 16].rearrange("(c p) t -> p c t", p=128), zt.rearrange("p (c t) -> p c t", t=8), r=["zt"], w=["UTpad"])
    ztb = AR.alloc([128, 32], BF16)
    vop(lambda e: e.memset(ztb, 0.0), w=["ztb"])
    dma(HF[:, 0:1].rearrange("(c p) t -> p c t", p=128), ztb[:, 0:16].rearrange("p (c t) -> p c t", t=1), r=["ztb"], w=["HFpad"], slow=True)
    dma(HF[:, S + 1:S + 2].rearrange("(c p) t -> p c t", p=128), ztb[:, 0:16].rearrange("p (c t) -> p c t", t=1), r=["ztb"], w=["HFpad"], slow=True)

    def wtiles(name):
        K, N = [(k, n) for (nm, k, n) in W_SPECS if nm == name][0]
        kp = (K + 2047) // 2048
        return kp, N // 512, K

    for l in range(L):
        for name, K, N in W_SPECS:
            kp, nn, _ = wtiles(name)
            for p_ in range(kp):
                k0 = p_ * 2048
                kc = min(16, (K - k0) // 128)
                for n in range(nn):
                    src = Wd[name][l, k0:k0 + kc * 128, n * 512:(n + 1) * 512].rearrange("(c p) j -> p c j", p=128)
                    dma(Wt[name][l, p_ * nn + n, :, 0:kc, :], src, w=[("W", name, l, p_ * nn + n)], q="pool", bg=True)
    P.barrier()

    def load_gT(dst, src_row):
        dma(dst, src_row.rearrange("(c p) -> p c", p=128), w=["gT"])

    def norm_transpose(xt_rows, rkeys, HT, htkey, col0, gTt, HB, hbkeys, ps_pool, nsub, ssq, rows=128):
        for s in range(nsub):
            xa = xt_rows(s)
            hb = HB[s % 2]
            hk = hbkeys[s % 2]
            sq = ssq[:, s:s + 1]
            act(hb, xa, AF.Square, r=[rkeys(s)], w=[hk, ("ssq", s)], accum_out=sq)
            rstd_ops(sq, sq, D, ("ssq", s))
            act(hb, xa, AF.Copy, r=[rkeys(s), ("ssq", s)], w=[hk], scale=sq)
            for half in range(2):
                b = ps_pool.next()
                pv = psbf(b).rearrange("p (c t) -> p c t", t=128)
                for c in range(8):
                    tr(pv[:, c, :], hb[:, (half * 8 + c) * 128:(half * 8 + c + 1) * 128], r=[hk], w=[("ps", b)])
                dst = HT[:, half * 8:half * 8 + 8, col0 + s * 128:col0 + (s + 1) * 128]
                g_b = gTt[:, half * 8:half * 8 + 8].unsqueeze(2).to_broadcast([128, 8, 128])
                vop(lambda e, dst=dst, pv=pv, g_b=g_b: e.tensor_tensor(out=dst, in0=pv, in1=g_b, op=ALU.mult),
                    r=[("ps", b), "gT"], w=[(htkey, s)])

    def load_w(WB, wkeys, name, l, tidx, kc=16):
        b, k = WB.next()
        dma(b[:, 0:kc, :], Wt[name][l, tidx, :, 0:kc, :], r=[("W", name, l, tidx)], w=[k])
        return b, k

    for l in range(L):
        lam_init = 0.8 - 0.6 * math.exp(-0.3 * l)
        xsrc = x_in if l == 0 else xs
        xdst = xs if l < L - 1 else y_out

        AR.reset()
        STG = AR.alloc([128, 128], F32)
        CSTG = AR.alloc([88, 4, 128], F32)
        vop(lambda e: e.memset(STG, 0.0), w=["STG"])
        for r0_, src_ in ((0, g_mix[l]), (16, g_cross[l]), (32, g_mem[l]), (48, g_ffn[l]), (64, gc_q[l]), (68, gc_k[l]), (72, pool_scale[l]), (80, g_sub[l])):
            nr = src_.shape[0] // 128
            dma(STG[r0_:r0_ + nr, :], src_.rearrange("(c p) -> c p", p=128), r=[], w=["STG"])
        P.op("pe", lambda e: e.transpose(psb[0][:, 0:128], STG, identf), r=["STG"], w=[("ps", 0)])
        act(gTall, psb[0][:, 0:128], AF.Copy, r=[("ps", 0)], w=["gT", "gsub", "pscale"])
        for j_ in range(3):
            dma(CSTG[:, j_, :], conv_w[l, j_].rearrange("(c p) -> c p", p=128), w=["CSTG"])
        dma(CSTG[:, 3, :], conv_b[l].rearrange("(c p) -> c p", p=128), w=["CSTG"])
        for j_ in range(4):
            P.op("pe", lambda e, j_=j_: e.transpose(psb[1][:, j_ * 88:(j_ + 1) * 88], CSTG[:, j_, :], identf[0:88, 0:88]), r=["CSTG"], w=[("ps", 1)])
        act(convw, psb[1][:, 0:264], AF.Copy, r=[("ps", 1)], w=["convw"])
        act(convb, psb[1][:, 264:352], AF.Copy, r=[("ps", 1)], w=["convw"])
        dma(gq_rep, g_q[l].partition_broadcast(128), w=["gq"])
        dma(gk_rep, g_k[l].partition_broadcast(128), w=["gq"])
        for i in range(4):
            dma(lamt[i], lam_in[i][l].partition_broadcast(128), w=["lam"])
        for i in range(2):
            vop(lambda e, i=i: e.tensor_tensor(out=lamtmp, in0=lamt[2 * i], in1=lamt[2 * i + 1], op=ALU.mult), r=["lam"], w=["lamtmp"])
            vop(lambda e, i=i: e.tensor_reduce(out=lamd[:, i:i + 1], in_=lamtmp, axis=AX.X, op=ALU.add), r=["lamtmp"], w=["lamd"])
        act(lamd[:, 2:4], lamd[:, 0:2], AF.Exp, r=["lamd"], w=["lamd"])
        vop(lambda e: e.tensor_tensor(out=nlam, in0=lamd[:, 3:4], in1=lamd[:, 2:3], op=ALU.subtract), r=["lamd"], w=["nlam"])
        vop(lambda e: e.tensor_scalar(out=nlam, in0=nlam, scalar1=-lam_init, scalar2=None, op0=ALU.add), r=["nlam"], w=["nlam"])
        vop(lambda e: e.tensor_scalar(out=gsubs, in0=gsub_raw, scalar1=1.0 - lam_init, scalar2=None, op0=ALU.mult), r=["gsub"], w=["gsubs"])
        P.barrier()

        AR.reset()
        WBl = [AR.alloc([128, 16, 512], BF16) for _ in range(3)]
        WB = Rot([(WBl[i], ("wb", i)) for i in range(3)])
        XR = AR.alloc([128, 4, D], F32)
        HB = [AR.alloc([128, D], BF16) for _ in range(2)]
        HT = AR.alloc([128, 16, 512], BF16)
        COS = AR.alloc([128, S // 128, 32], F32)
        SIN = AR.alloc([128, S // 128, 32], F32)
        SQ = [AR.alloc([128, 512], F32) for _ in range(2)]
        QN = [AR.alloc([128, 512], F32) for _ in range(2)]
        RT = [AR.alloc([128, 256], F32) for _ in range(4)]
        QR = [AR.alloc([128, 512], BF16) for _ in range(2)]
        QTs = AR.alloc([128, 16, 512], BF16)
        Vs = AR.alloc([128, 4, 1024], BF16)
        UTs = AR.alloc([128, 8, 512], F32)
        ssq = smalls[:, 0:4]
        qss = [smalls[:, 8:16], smalls[:, 16:24]]
        dma(COS, cos_in.rearrange("(s p) j -> p s j", p=128), w=["cos"])
        dma(SIN, sin_in.rearrange("(s p) j -> p s j", p=128), w=["cos"])
        psG = Rot([0, 1, 2, 3])
        psT = Rot([4, 5])
        psQ = Rot([6, 7])
        evi = 0
        pendA = []
        for ti in range(NT):
            t0 = ti * 512
            for s in range(4):
                dma(XR[:, s, :], xsrc[t0 + s * 128:t0 + (s + 1) * 128, :], w=[("XR", s)])
            norm_transpose(lambda s: XR[:, s, :], lambda s: ("XR", s), HT, "HT", 0, gT["mix"], HB, [("HB", 0), ("HB", 1)], psT, 4, ssq)
            htk = [("HT", s) for s in range(4)]
            for n in range(6):
                wb, wk = load_w(WB, None, "w_in", l, n)
                for s in range(4):
                    b = psG.next()
                    for k in range(16):
                        mm(psb[b][:], HT[:, k, s * 128:(s + 1) * 128], wb[:, k, :], k == 0, k == 15, r=[htk[s], wk], w=[("ps", b)])
                    while pendA:
                        pendA.pop(0)()
                    if n < 4:
                        e_ = evi % 2
                        evi += 1
                        sq_, qn_, qr_, qs_ = SQ[e_], QN[e_], QR[e_], qss[e_]
                        kq, kn, kr, ks = ("SQ", e_), ("QN", e_), ("QR", e_), ("qss", e_)
                        grep = gq_rep if n < 2 else gk_rep
                        act(sq_, psb[b][:], AF.Square, r=[("ps", b)], w=[kq])
                        vop(lambda e, sq_=sq_, qs_=qs_: e.tensor_reduce(out=qs_, in_=sq_.rearrange("p (g d) -> p g d", d=64), axis=AX.X, op=ALU.add), r=[kq], w=[ks])
                        rstd_ops(qs_, qs_, 64, ks)
                        pv3 = psb[b][:].rearrange("p (g d) -> p g d", d=64)
                        qn3 = qn_.rearrange("p (g d) -> p g d", d=64)
                        vop(lambda e, qn3=qn3, pv3=pv3, qs_=qs_: e.tensor_tensor(out=qn3, in0=pv3, in1=qs_.unsqueeze(2).to_broadcast([128, 8, 64]), op=ALU.mult),
                            r=[("ps", b), ks], w=[kn])
                        vop(lambda e, qn3=qn3, grep=grep: e.tensor_tensor(out=qn3, in0=qn3, in1=grep.unsqueeze(1).to_broadcast([128, 8, 64]), op=ALU.mult),
                            r=[kn, "gq"], w=[kn])
                        x1 = qn3[:, :, 0:32]
                        x2 = qn3[:, :, 32:64]
                        cs = COS[:, ti * 4 + s, :].unsqueeze(1).to_broadcast([128, 8, 32])
                        sn = SIN[:, ti * 4 + s, :].unsqueeze(1).to_broadcast([128, 8, 32])
                        r3 = [t.rearrange("p (g d) -> p g d", d=32) for t in RT]
                        qr3 = qr_.rearrange("p (g d) -> p g d", d=64)
                        vop(lambda e, x1=x1, cs=cs, r3=r3: e.tensor_tensor(out=r3[0], in0=x1, in1=cs, op=ALU.mult), r=[kn, "cos"], w=[("RT", 0)])
                        vop(lambda e, x2=x2, sn=sn, r3=r3: e.tensor_tensor(out=r3[1], in0=x2, in1=sn, op=ALU.mult), r=[kn, "cos"], w=[("RT", 1)])
                        vop(lambda e, r3=r3, qr3=qr3: e.tensor_tensor(out=qr3[:, :, 0:32], in0=r3[0], in1=r3[1], op=ALU.subtract), r=[("RT", 0), ("RT", 1)], w=[kr])
                        vop(lambda e, x2=x2, cs=cs, r3=r3: e.tensor_tensor(out=r3[2], in0=x2, in1=cs, op=ALU.mult), r=[kn, "cos"], w=[("RT", 2)])
                        vop(lambda e, x1=x1, sn=sn, r3=r3: e.tensor_tensor(out=r3[3], in0=x1, in1=sn, op=ALU.mult), r=[kn, "cos"], w=[("RT", 3)])
                        vop(lambda e, r3=r3, qr3=qr3: e.tensor_tensor(out=qr3[:, :, 32:64], in0=r3[2], in1=r3[3], op=ALU.add), r=[("RT", 2), ("RT", 3)], w=[kr])
                        def fin(qr_=qr_, kr=kr, n=n, s=s):
                            bq = psQ.next()
                            pq = psbf(bq).rearrange("p (c t) -> p c t", t=128)
                            for hh in range(4):
                                tr(pq[:, hh, :], qr_[:, hh * 128:(hh + 1) * 128], r=[kr], w=[("ps", bq)])
                            act(QTs[:, n * 4:n * 4 + 4, s * 128:(s + 1) * 128], pq[:, 0:4, :], AF.Copy, r=[("ps", bq)], w=[("QTs", n // 2)])
                        pendA.append(fin)
                    else:
                        act(Vs[:, s, (n - 4) * 512:(n - 3) * 512], psb[b][:], AF.Copy, r=[("ps", b)], w=["Vs"])
            while pendA:
                pendA.pop(0)()
            dma(QT[:, :, t0:t0 + 512].rearrange("h p t -> p h t"), QTs[:, 0:8, :], r=[("QTs", 0)], w=[("QT", ti)])
            dma(KT[:, :, t0:t0 + 512].rearrange("h p t -> p h t"), QTs[:, 8:16, :], r=[("QTs", 1)], w=[("KT", ti)])
            dma(Vd[t0:t0 + 512, :].rearrange("(s p) n -> p s n", p=128), Vs, r=["Vs"], w=[("Vd", ti)])
            for n in range(6, 8):
                wb, wk = load_w(WB, None, "w_in", l, n)
                for sc in range(4):
                    b = psG.next()
                    for k in range(16):
                        mm(psb[b][:], wb[:, k, sc * 128:(sc + 1) * 128], HT[:, k, :], k == 0, k == 15, r=htk + [wk], w=[("ps", b)])
                    act(UTs[:, (n - 6) * 4 + sc, :], psb[b][:], AF.Copy, r=[("ps", b)], w=["UTs"])
            dma(UT[:, 8 + t0:8 + t0 + 512].rearrange("(c p) t -> p c t", p=128), UTs, r=["UTs"], w=[("UT", ti)])
        P.barrier()

        AR.reset()
        KTb = [AR.alloc([128, S], BF16) for _ in range(2)]
        QTb = [AR.alloc([128, S], BF16) for _ in range(2)]
        Vb = [AR.alloc([128, NKT, 128], BF16) for _ in range(2)]
        Pt = [AR.alloc([128, 1024], BF16) for _ in range(3)]
        Pr = Rot([(Pt[i], ("P", i)) for i in range(3)])
        R0 = AR.alloc([128, 512], F32); R1 = AR.alloc([128, 512], F32)
        T0 = AR.alloc([128, 512], F32); T1 = AR.alloc([128, 512], F32)
        OC = AR.alloc([128, 512], F32); RS = AR.alloc([128, 512], F32)
        SQb = AR.alloc([128, 512], BF16)
        AO = [AR.alloc([128, 512], BF16) for _ in range(2)]
        psS = Rot([0, 1])
        scale = 64 ** -0.5
        it = 0

        def head_loads(h):
            hb_ = h % 2
            dma(KTb[hb_], KT[h], r=[("KT", ti) for ti in range(NT)], w=[("KTb", hb_)])
            dma(QTb[hb_], QT[h], r=[("QT", ti) for ti in range(NT)], w=[("QTb", hb_)])
            dma(Vb[hb_], Vd[:, h * 128:(h + 1) * 128].rearrange("(kt p) d -> p kt d", p=128), r=[("Vd", ti) for ti in range(NT)], w=[("Vb", hb_)])

        def issue_qk(step):
            h, qc, kt = step
            hb_ = h % 2
            q0 = qc * 512
            pi = psS.next()
            for c in range(2):
                mm(psb[2 * pi + c][:], KTb[hb_][c * 64:(c + 1) * 64, kt * 128:(kt + 1) * 128], QTb[hb_][c * 64:(c + 1) * 64, q0:q0 + 512], True, True,
                   r=[("KTb", hb_), ("QTb", hb_)], w=[("pp", pi)])
            return pi

        def epilogue2(h, q0, qc):
            nonlocal_it = it_box[0]
            pi = psS.next()
            psS.next()
            mm(psb[2 * pi][:], onesb[:], SQb, True, True, r=["SQb"], w=[("pp", pi)])
            vop(lambda e: e.tensor_scalar(out=RS, in0=psb[2 * pi][:], scalar1=1.0 / 128, scalar2=EPS, op0=ALU.mult, op1=ALU.add), r=[("pp", pi)], w=["RS"])
            act(RS, RS, AF.Sqrt, r=["RS"], w=["RS"])
            vop(lambda e: e.reciprocal(out=RS, in_=RS), r=["RS"], w=["RS"])
            ao = AO[nonlocal_it % 2]
            ak = ("AO", nonlocal_it % 2)
            it_box[0] += 1
            vop(lambda e: e.scalar_tensor_tensor(out=ao, in0=OC, scalar=gsubs, in1=RS, op0=ALU.mult, op1=ALU.mult), r=["OC", "RS"], w=[ak])
            dma(AT[h * 128:(h + 1) * 128, q0:q0 + 512], ao, r=[ak], w=[("AT", qc)])

        it_box = [0]
        steps = [(h, qc, kt) for h in range(NH) for qc in range(NT) for kt in range(NKT)]
        deferred = []
        head_loads(0)
        pi_cur = issue_qk(steps[0])
        for i, (h, qc, kt) in enumerate(steps):
            hb_ = h % 2
            q0 = qc * 512
            if qc == 0 and kt == 0 and h + 1 < NH:
                head_loads(h + 1)
            pi_next = issue_qk(steps[i + 1]) if i + 1 < len(steps) else None
            qhalf = 0 if q0 < HALF else 1
            khalf = 0 if kt * 128 < HALF else 1
            mb = maskb[:, 0:1] if khalf == qhalf else maskb[:, 1:2]
            pt, pk = Pr.next()
            act(pt, pp[pi_cur][:], AF.Exp, r=[("pp", pi_cur)], w=[pk], bias=mb, scale=scale)
            for c in range(2):
                mm(psb[4 + c][:], Vb[hb_][:, kt, :], pt[:, c * 512:(c + 1) * 512], kt == 0, kt == NKT - 1, r=[("Vb", hb_), pk], w=[("ps", 4 + c)])
                mm(psb[6 + c][:], onesb[:], pt[:, c * 512:(c + 1) * 512], kt == 0, kt == NKT - 1, r=[pk], w=[("ps", 6 + c)])
            while deferred and deferred[0][0] <= i:
                deferred.pop(0)[1]()
            if kt == NKT - 1:
                vop(lambda e: e.tensor_copy(out=T0, in_=psb[4][:]), r=[("ps", 4)], w=["T0"])
                act(R0, psb[6][:], AF.Copy, r=[("ps", 6)], w=["R0"])
                vop(lambda e: e.tensor_copy(out=T1, in_=psb[5][:]), r=[("ps", 5)], w=["T1"])
                act(R1, psb[7][:], AF.Copy, r=[("ps", 7)], w=["R1"])
                vop(lambda e: e.reciprocal(out=R0, in_=R0), r=["R0"], w=["R0"])
                vop(lambda e: e.reciprocal(out=R1, in_=R1), r=["R1"], w=["R1"])
                vop(lambda e: e.tensor_tensor(out=T0, in0=T0, in1=R0, op=ALU.mult), r=["T0", "R0"], w=["T0"])
                vop(lambda e: e.tensor_tensor(out=T1, in0=T1, in1=R1, op=ALU.mult), r=["T1", "R1"], w=["T1"])
                vop(lambda e: e.scalar_tensor_tensor(out=OC, in0=T1, scalar=nlam, in1=T0, op0=ALU.mult, op1=ALU.add), r=["T0", "T1"], w=["OC"])
                act(SQb, OC, AF.Square, r=["OC"], w=["SQb"])
                deferred.append((i + 8, (lambda h=h, q0=q0, qc=qc: epilogue2(h, q0, qc))))
            pi_cur = pi_next
        while deferred:
            deferred.pop(0)[1]()
        P.barrier()

        AR.reset()
        WBl = [AR.alloc([128, 16, 512], BF16) for _ in range(3)]
        WB = Rot([(WBl[i], ("wb", i)) for i in range(3)])
        XR = AR.alloc([128, 4, D], F32)
        HB = [AR.alloc([128, D], BF16) for _ in range(2)]
        HTa = AR.alloc([128, 16, 512], BF16)
        HTb = AR.alloc([128, 16, 512], BF16)
        CKs = AR.alloc([128, 16, 256], BF16)
        CVs = AR.alloc([128, 2, 2048], BF16)
        UTt = [AR.alloc([128, 2, 528], F32) for _ in range(2)]
        PW = [AR.alloc([128, 2, 528], F32) for _ in range(2)]
        ZT = [AR.alloc([128, 2, 512], BF16) for _ in range(2)]
        ICN = AR.alloc([128, 4, 512], F32)
        WP = AR.alloc([128, 8, 256], BF16)
        CQN = [AR.alloc([128, 512], BF16) for _ in range(2)]
        CQT = AR.alloc([128, 4, 512], BF16)
        PC = [AR.alloc([128, 512], BF16) for _ in range(4)]
        RC = AR.alloc([128, 512], F32)
        ssq = smalls[:, 0:4]
        cqs = [smalls[:, 8:9], smalls[:, 9:10]]
        psG = Rot([0, 1, 2, 3])
        psT = Rot([4, 5])
        psQ = Rot([6, 7])
        dma(WP, w_pool[l].rearrange("g (cc p) d -> p (g cc) d", p=128), w=["WP"], q="pool")
        for hf in range(2):
            for s in range(2):
                dma(XR[:, s, :], mem_in[hf, s * 128:(s + 1) * 128, :], w=[("XR", s)])
            norm_transpose(lambda s: XR[:, s, :], lambda s: ("XR", s), HTa, "HTa", 0, gT["mem"], HB, [("HB", 0), ("HB", 1)], psT, 2, ssq)
            hk = [("HTa", 0), ("HTa", 1)]
            for n in range(8):
                wb, wk = load_w(WB, None, "wc_kv", l, n)
                for s in range(2):
                    b = psG.next()
                    for k in range(16):
                        mm(psb[b][:], HTa[:, k, s * 128:(s + 1) * 128], wb[:, k, :], k == 0, k == 15, r=[hk[s], wk], w=[("ps", b)])
                    if n < 4:
                        e_ = (n * 2 + s) % 2
                        act(CQN[e_], psb[b][:], AF.Square, r=[("ps", b)], w=[("CQN", e_), ("cqs", e_)], accum_out=cqs[e_])
                        rstd_ops(cqs[e_], cqs[e_], 512, ("cqs", e_))
                        act(CQN[e_], psb[b][:], AF.Copy, r=[("ps", b), ("cqs", e_)], w=[("CQN", e_)], scale=cqs[e_])
                        bq = psQ.next()
                        pq = psbf(bq).rearrange("p (c t) -> p c t", t=128)
                        for dc in range(4):
                            tr(pq[:, dc, :], CQN[e_][:, dc * 128:(dc + 1) * 128], r=[("CQN", e_)], w=[("ps", bq)])
                        dstk = CKs[:, n * 4:n * 4 + 4, s * 128:(s + 1) * 128]
                        g_b = gckT[:, 0:4].unsqueeze(2).to_broadcast([128, 4, 128])
                        vop(lambda e, dstk=dstk, pq=pq, g_b=g_b: e.tensor_tensor(out=dstk, in0=pq[:, 0:4, :], in1=g_b, op=ALU.mult), r=[("ps", bq), "gT"], w=["CKs"])
                    else:
                        act(CVs[:, s, (n - 4) * 512:(n - 3) * 512], psb[b][:], AF.Copy, r=[("ps", b)], w=["CVs"])
            dma(CK[hf], CKs, r=["CKs"], w=[("CK", hf)])
            dma(CV[hf], CVs, r=["CVs"], w=[("CV", hf)])
        for ti in range(NT):
            t0 = ti * 512
            hf = 0 if t0 < HALF else 1
            for s in range(4):
                dma(XR[:, s, :], xsrc[t0 + s * 128:t0 + (s + 1) * 128, :], w=[("XR", s)])
            dma(CKs, CK[hf], r=[("CK", hf)], w=["CKs"])
            dma(CVs, CV[hf], r=[("CV", hf)], w=["CVs"])
            dma(ICN, invcnt_in[:, t0:t0 + 512].partition_broadcast(128), w=["ICN"])
            dma(HTa[:, 0:8, :], AT[:, t0:t0 + 512].rearrange("(c p) t -> p c t", p=128), r=[("AT", ti)], w=[("HTa", c) for c in range(8)])
            for g in range(4):
                wdw = 2 << g
                u = UTt[g % 2]
                uk = ("UTt", g % 2)
                dma(u, UT[g * 256:(g + 1) * 256, t0:t0 + 528].rearrange("(c p) t -> p c t", p=128),
                    r=[("UT", tj) for tj in range(max(0, ti - 1), min(NT, ti + 2))] + ["UTpad"], w=[uk])
                if t0 + 512 == HALF:
                    vop(lambda e, u=u: e.tensor_scalar(out=u[:, :, 520:528], in0=u[:, :, 520:528], scalar1=bvec, scalar2=None, op0=ALU.mult), r=[uk], w=[uk])
                if t0 == HALF:
                    vop(lambda e, u=u: e.tensor_scalar(out=u[:, :, 0:8], in0=u[:, :, 0:8], scalar1=bvec, scalar2=None, op0=ALU.mult), r=[uk], w=[uk])
                a_, b_ = PW[0], PW[1]
                vop(lambda e, u=u, a_=a_: e.tensor_tensor(out=a_[:, :, 1:528], in0=u[:, :, 0:527], in1=u[:, :, 1:528], op=ALU.add), r=[uk], w=["PW0"])
                cur, curk, oth, othk = a_, "PW0", b_, "PW1"
                lo, hi = 1, 528
                sh = 1
                for step in range(g):
                    nlo, nhi = lo + sh, hi - sh
                    vop(lambda e, cur=cur, oth=oth, nlo=nlo, nhi=nhi, sh=sh: e.tensor_tensor(out=oth[:, :, nlo:nhi], in0=cur[:, :, nlo - sh:nhi - sh], in1=cur[:, :, nlo + sh:nhi + sh], op=ALU.add),
                        r=[curk], w=[othk])
                    cur, curk, oth, othk = oth, othk, cur, curk
                    lo, hi = nlo, nhi
                    sh *= 2
                assert lo <= 8 and hi >= 520
                icb = ICN[:, g, :].unsqueeze(1).to_broadcast([128, 2, 512])
                vop(lambda e, cur=cur, oth=oth, icb=icb: e.tensor_tensor(out=oth[:, :, 8:520], in0=cur[:, :, 8:520], in1=icb, op=ALU.mult), r=[curk, "ICN"], w=[othk])
                z = ZT[g % 2]
                zk = ("ZT", g % 2)
                vop(lambda e, oth=oth, u=u, z=z: e.tensor_tensor(out=z, in0=oth[:, :, 8:520], in1=u[:, :, 8:520], op=ALU.subtract), r=[othk, uk], w=[zk])
                for dd in range(2):
                    b = psQ.next()
                    for cc in range(2):
                        mm(psb[b][:], WP[:, g * 2 + cc, dd * 128:(dd + 1) * 128], z[:, cc, :], cc == 0, cc == 1, r=[zk, "WP"], w=[("ps", b)])
                    act(HTa[:, 8 + 2 * g + dd, :], psb[b][:], AF.Copy, r=[("ps", b), "pscale"], w=[("HTa", 8 + 2 * g + dd)], scale=pscale[:, 2 * g + dd:2 * g + dd + 1])
            mk = [("HTa", i) for i in range(16)]
            for n in range(4):
                wb, wk = load_w(WB, None, "w_out", l, n)
                for s in range(4):
                    b = psG.next()
                    for k in range(16):
                        mm(psb[b][:], HTa[:, k, s * 128:(s + 1) * 128], wb[:, k, :], k == 0, k == 15, r=mk + [wk], w=[("ps", b)])
                    xa = XR[:, s, n * 512:(n + 1) * 512]
                    vop(lambda e, xa=xa, b=b: e.tensor_tensor(out=xa, in0=psb[b][:], in1=xa, op=ALU.add), r=[("ps", b), ("XR", s)], w=[("XR", s)])
            norm_transpose(lambda s: XR[:, s, :], lambda s: ("XR", s), HTb, "HTb", 0, gT["cross"], HB, [("HB", 0), ("HB", 1)], psT, 4, ssq)
            hbk = [("HTb", s) for s in range(4)]
            ei = 0
            pendC = []
            for hh in range(4):
                wb, wk = load_w(WB, None, "wc_q", l, hh)
                for s in range(4):
                    b = psG.next()
                    for k in range(16):
                        mm(psb[b][:], HTb[:, k, s * 128:(s + 1) * 128], wb[:, k, :], k == 0, k == 15, r=[hbk[s], wk], w=[("ps", b)])
                    while pendC:
                        pendC.pop(0)()
                    e_ = ei % 2
                    ei += 1
                    act(CQN[e_], psb[b][:], AF.Square, r=[("ps", b)], w=[("CQN", e_), ("cqs", e_)], accum_out=cqs[e_])
                    rstd_ops(cqs[e_], cqs[e_], 512, ("cqs", e_))
                    act(CQN[e_], psb[b][:], AF.Copy, r=[("ps", b), ("cqs", e_)], w=[("CQN", e_)], scale=cqs[e_])
                    def finq(e_=e_, s=s):
                        bq = psQ.next()
                        pq = psbf(bq).rearrange("p (c t) -> p c t", t=128)
                        for dc in range(4):
                            tr(pq[:, dc, :], CQN[e_][:, dc * 128:(dc + 1) * 128], r=[("CQN", e_)], w=[("ps", bq)])
                        dstq = CQT[:, 0:4, s * 128:(s + 1) * 128]
                        g_b = gcqT[:, 0:4].unsqueeze(2).to_broadcast([128, 4, 128])
                        vop(lambda e: e.tensor_tensor(out=dstq, in0=pq[:, 0:4, :], in1=g_b, op=ALU.mult), r=[("ps", bq), "gT"], w=["CQT"])
                    pendC.append(finq)
                while pendC:
                    pendC.pop(0)()
                pcs = []
                for mc in range(2):
                    b = psG.next()
                    for dc in range(4):
                        mm(psb[b][:], CKs[:, hh * 4 + dc, mc * 128:(mc + 1) * 128], CQT[:, dc, :], dc == 0, dc == 3, r=["CKs", "CQT"], w=[("ps", b)])
                    pc = PC[(hh * 2 + mc) % 4]
                    pk = ("PC", (hh * 2 + mc) % 4)
                    act(pc, psb[b][:], AF.Exp, r=[("ps", b)], w=[pk], scale=512 ** -0.5)
                    pcs.append((pc, pk))
                bz = psG.next()
                for mc in range(2):
                    mm(psb[bz][:], onesb[:], pcs[mc][0], mc == 0, mc == 1, r=[pcs[mc][1]], w=[("ps", bz)])
                vop(lambda e, bz=bz: e.reciprocal(out=RC, in_=psb[bz][:]), r=[("ps", bz)], w=["RC"])
                for dvc in range(4):
                    b = psG.next()
                    for mc in range(2):
                        mm(psb[b][:], CVs[:, mc, hh * 512 + dvc * 128:hh * 512 + (dvc + 1) * 128], pcs[mc][0], mc == 0, mc == 1, r=["CVs", pcs[mc][1]], w=[("ps", b)])
                    dsto = HTa[:, hh * 4 + dvc, :]
                    vop(lambda e, dsto=dsto, b=b: e.tensor_tensor(out=dsto, in0=psb[b][:], in1=RC, op=ALU.mult), r=[("ps", b), "RC"], w=[("HTa", hh * 4 + dvc)])
            ok = mk
            for n in range(4):
                wb, wk = load_w(WB, None, "wc_o", l, n)
                for s in range(4):
                    b = psG.next()
                    for k in range(16):
                        mm(psb[b][:], HTa[:, k, s * 128:(s + 1) * 128], wb[:, k, :], k == 0, k == 15, r=ok + [wk], w=[("ps", b)])
                    xa = XR[:, s, n * 512:(n + 1) * 512]
                    vop(lambda e, xa=xa, b=b: e.tensor_tensor(out=xa, in0=psb[b][:], in1=xa, op=ALU.add), r=[("ps", b), ("XR", s)], w=[("XR", s)])
            for s in range(4):
                dma(xmid[t0 + s * 128:t0 + (s + 1) * 128, :], XR[:, s, :], r=[("XR", s)], w=[("xmid", ti)])
            norm_transpose(lambda s: XR[:, s, :], lambda s: ("XR", s), HTb, "HTb", 0, gT["ffn"], HB, [("HB", 0), ("HB", 1)], psT, 4, ssq)
            dma(HF[:, 1 + t0:1 + t0 + 512].rearrange("(c p) t -> p c t", p=128), HTb, r=hbk, w=[("HF", ti)])
        P.barrier()

        AR.reset()
        WBl = [AR.alloc([128, 16, 512], BF16) for _ in range(4)]
        WB = Rot([(WBl[i], ("wb", i)) for i in range(4)])
        HTf = [AR.alloc([128, 16, 512], BF16) for _ in range(2)]
        ACTT = AR.alloc([128, 44, 512], BF16)
        XF = [AR.alloc([128, D], F32) for _ in range(4)]
        CA = [AR.alloc([128, 512], F32) for _ in range(4)]
        GL = [AR.alloc([128, 512], F32) for _ in range(2)]
        psU = Rot([0, 1, 2, 3])
        psD = [4, 5, 6, 7]
        hfk_all = [("HF", tj) for tj in range(NT)] + ["HFpad"]
        xi = 0
        for fi, (ts, tn) in enumerate(ffn_tiles(S)):
            wcols = tn + 2
            ht = HTf[fi % 2]
            hk_ = ("HTf", fi % 2)
            dma(ht[:, :, 0:wcols], HF[:, ts:ts + wcols].rearrange("(c p) t -> p c t", p=128), r=hfk_all, w=[hk_])
            if ts + tn == HALF:
                vop(lambda e, ht=ht, wcols=wcols: e.tensor_scalar(out=ht[:, :, wcols - 1:wcols], in0=ht[:, :, wcols - 1:wcols], scalar1=bvec, scalar2=None, op0=ALU.mult), r=[hk_], w=[hk_])
            if ts == HALF:
                vop(lambda e, ht=ht: e.tensor_scalar(out=ht[:, :, 0:1], in0=ht[:, :, 0:1], scalar1=bvec, scalar2=None, op0=ALU.mult), r=[hk_], w=[hk_])
            for i in range(11):
                wg, wgk = load_w(WB, None, "w_up", l, i)
                wv, wvk = load_w(WB, None, "w_up", l, 11 + i)
                for sc in range(4):
                    j = i * 4 + sc
                    res = []
                    for which, (wb, wk) in enumerate(((wg, wgk), (wv, wvk))):
                        b = psU.next()
                        for k in range(16):
                            mm(psb[b][:, 0:wcols], wb[:, k, sc * 128:(sc + 1) * 128], ht[:, k, 0:wcols], k == 0, k == 15, r=[hk_, wk], w=[("ps", b)])
                        ch = j if which == 0 else 44 + j
                        ca = CA[which * 2 + (j % 2)]
                        cak = ("CA", which * 2 + (j % 2))
                        act(ca[:, 0:tn], psb[b][:, 1:1 + tn], AF.Identity, r=[("ps", b), "convw"], w=[cak],
                            bias=convb[:, ch:ch + 1], scale=convw[:, 88 + ch:88 + ch + 1])
                        vop(lambda e, ca=ca, b=b, ch=ch: e.scalar_tensor_tensor(out=ca[:, 0:tn], in0=psb[b][:, 0:tn], scalar=convw[:, ch:ch + 1], in1=ca[:, 0:tn], op0=ALU.mult, op1=ALU.add),
                            r=[("ps", b), cak], w=[cak])
                        vop(lambda e, ca=ca, b=b, ch=ch: e.scalar_tensor_tensor(out=ca[:, 0:tn], in0=psb[b][:, 2:2 + tn], scalar=convw[:, 176 + ch:176 + ch + 1], in1=ca[:, 0:tn], op0=ALU.mult, op1=ALU.add),
                            r=[("ps", b), cak], w=[cak])
                        res.append((ca, cak))
                    gl = GL[j % 2]
                    glk = ("GL", j % 2)
                    act(gl[:, 0:tn], res[0][0][:, 0:tn], AF.Gelu_apprx_tanh, r=[res[0][1]], w=[glk])
                    vop(lambda e, gl=gl, cv_=res[1][0], j=j: e.tensor_tensor(out=ACTT[:, j, 0:tn], in0=gl[:, 0:tn], in1=cv_[:, 0:tn], op=ALU.mult),
                        r=[glk, res[1][1]], w=[("ACTT", j)], eng="pool")
            ak_all = [("ACTT", j) for j in range(44)]
            subs = []
            o = 0
            while o < tn:
                subs.append((o, min(128, tn - o)))
                o += 128
            for n in range(4):
                for p_ in range(3):
                    kc = 16 if p_ < 2 else 12
                    wb, wk = load_w(WB, None, "w_down", l, p_ * 4 + n, kc=kc)
                    for si, (so, sn_) in enumerate(subs):
                        b = psD[si]
                        for k in range(kc):
                            kk = p_ * 16 + k
                            mm(psb[b][0:sn_, :], ACTT[:, kk, so:so + sn_], wb[:, k, :], kk == 0, kk == 43, r=ak_all + [wk], w=[("ps", b)])
                for si, (so, sn_) in enumerate(subs):
                    b = psD[si]
                    if n == 0:
                        pass
                    xf = XF[si]
                    xfk = ("XF", si)
                    if n == 0:
                        dma(xf[0:sn_, :], xmid[ts + so:ts + so + sn_, :], r=[("xmid", tj) for tj in range(NT)], w=[xfk])
                    xa = xf[0:sn_, n * 512:(n + 1) * 512]
                    vop(lambda e, xa=xa, b=b, sn_=sn_: e.tensor_tensor(out=xa, in0=psb[b][0:sn_, :], in1=xa, op=ALU.add), r=[("ps", b), xfk], w=[xfk])
                    if n == 3:
                        dma(xdst[ts + so:ts + so + sn_, :], xf[0:sn_, :], r=[xfk], w=[("xout", fi, si)])
            xi += len(subs)
        P.barrier()

    P.emit(nc, stack)
    stack.close()
    return nc


def core_tables(S, split):
    half = S // 2
    pos = np.arange(S)
    if split:
        pos = pos % half
        seqlen = half
    else:
        seqlen = S
    inv = (10000.0 ** (-np.arange(32, dtype=np.float32) / 32)).astype(np.float32)
    ang = pos.astype(np.float32)[:, None] * inv[None, :]
    cos = np.cos(ang).astype(np.float32)
    sin = np.sin(ang).astype(np.float32)
    invcnt = np.zeros((4, S), np.float32)
    for g, w in enumerate((2, 4, 8, 16)):
        lo = np.clip(pos - w // 2, 0, seqlen)
        hi = np.clip(pos + w // 2, 0, seqlen)
        invcnt[g] = 1.0 / (hi - lo).astype(np.float32)
    maskb = np.zeros((128, 2), np.float32)
    maskb[:, 1] = NEG if split else 0.0
    bvec = np.full((128, 1), 0.0 if split else 1.0, np.float32)
    return cos, sin, invcnt, maskb, bvec


_NC_CACHE = {}


def run_cores(core_specs, weights, S, L):
    key = (S, L)
    if key not in _NC_CACHE:
        _NC_CACHE[key] = build_program(S, L)
    nc = _NC_CACHE[key]
    ident = np.eye(128, dtype=np.float32)
    in_maps = []
    for x, mem, split in core_specs:
        cos, sin, invcnt, maskb, bvec = core_tables(S, split)
        m = {"x": np.ascontiguousarray(x, dtype=np.float32), "mem": np.ascontiguousarray(mem, dtype=np.float32),
             "rope_cos": cos, "rope_sin": sin, "invcnt": invcnt, "maskb": maskb, "bvec": bvec, "ident": ident}
        for k, v in weights.items():
            m[k] = np.ascontiguousarray(v, dtype=np.float32)
        in_maps.append(m)
    res = run_bass_kernel_spmd(nc, in_maps, core_ids=list(range(len(core_specs))))
    return [r["y"] for r in res.results]


WEIGHT_NAMES = ["g_mix", "w_in", "g_q", "g_k", "lam_q1", "lam_k1", "lam_q2", "lam_k2", "g_sub", "w_pool", "pool_scale",
                "w_out", "g_cross", "g_mem", "wc_q", "wc_kv", "gc_q", "gc_k", "wc_o", "g_ffn", "w_up", "conv_w", "conv_b", "w_down"]


def kernel(**inputs):
    xp = np.asarray(inputs["x_prompt"]); xsm = np.asarray(inputs["x_sample"])
    mp = np.asarray(inputs["mem_prompt"]); ms = np.asarray(inputs["mem_sample"])
    weights = {k: np.asarray(inputs[k]) for k in WEIGHT_NAMES}
    L = weights["w_in"].shape[0]
    S = xp.shape[1]
    specs = []
    for i in range(4):
        specs.append((xp[i], np.stack([mp[i], mp[i]]), False))
    for j in range(4):
        specs.append((xsm[2 * j:2 * j + 2].reshape(S, D), ms[2 * j:2 * j + 2], True))
    ys = run_cores(specs, weights, S, L)
    y_prompt = np.stack(ys[0:4]).astype(np.float32)
    y_sample = np.concatenate([y.reshape(2, S // 2, D) for y in ys[4:8]], axis=0).astype(np.float32)
    return (y_prompt, y_sample)
```

```python
import contextlib
import math
import numpy as np
import concourse.bass as bass
import concourse.mybir as mybir
from concourse.bass_utils import run_bass_kernel_spmd

F32 = mybir.dt.float32
BF16 = mybir.dt.bfloat16
AF = mybir.ActivationFunctionType
ALU = mybir.AluOpType
AX = mybir.AxisListType

D = 2048
NH = 8
N_MEM = 256
D_FF = 5632
EPS = 1e-6
NEG = -30000.0


class Op:
    __slots__ = ("eng", "fn", "dma", "deps", "inc", "sem", "val", "bar", "pos", "q")


class Prog:
    CE = ("pe", "act", "dve", "pool")
    ALL = ("pe", "act", "dve", "pool", "sp")

    def __init__(self):
        self.ops = []
        self.last_w = {}
        self.readers = {}
        self.stream = {e: [] for e in self.ALL}
        self.persist = set()

    def op(self, eng, fn, r=(), w=(), dma=False, bg=False):
        o = Op()
        o.eng, o.fn, o.dma, o.inc, o.sem, o.val, o.bar = eng, fn, dma, False, None, 0, None
        o.q = ("bg" if bg else eng) if dma else None
        idx = len(self.ops)
        deps = set()
        for k in r:
            lw = self.last_w.get(k)
            if lw is not None:
                deps.add(lw)
        for k in w:
            lw = self.last_w.get(k)
            if lw is not None:
                deps.add(lw)
            rd = self.readers.get(k)
            if rd:
                deps.update(rd.values())
        for k in r:
            rd = self.readers.setdefault(k, {})
            rd[("d", idx) if dma else eng] = idx
        for k in w:
            self.last_w[k] = idx
            self.readers[k] = {}
            if bg:
                self.persist.add(k)
        best = {}
        keep = []
        for d in deps:
            od = self.ops[d]
            if od.dma:
                keep.append(d)
            else:
                if od.eng == eng and eng == "pe":
                    continue
                if od.eng not in best or best[od.eng] < d:
                    best[od.eng] = d
        keep.extend(best.values())
        o.deps = keep
        for d in keep:
            self.ops[d].inc = True
        o.pos = len(self.stream[eng])
        self.ops.append(o)
        self.stream[eng].append(idx)
        return idx

    def barrier(self):
        lasts = []
        for e in self.ALL:
            for li in reversed(self.stream[e]):
                ol = self.ops[li]
                if ol.bar:
                    break
                if not ol.dma:
                    lasts.append(li)
                    break
        for e in self.ALL:
            o = Op()
            o.eng, o.fn, o.dma, o.inc, o.sem, o.val, o.pos = e, None, False, False, None, 0, len(self.stream[e])
            o.q = None
            o.deps = []
            o.bar = True
            for li in lasts:
                ol = self.ops[li]
                if not ol.dma and ol.eng != e and ol.fn is not None:
                    ol.inc = True
                    o.deps.append(li)
            self.ops.append(o)
            self.stream[e].append(len(self.ops) - 1)
        self.last_w = {k: v for k, v in self.last_w.items() if k in self.persist}
        self.readers = {k: {} for k in self.last_w}

    def emit(self, nc, stack):
        NS = 12
        MAXC = 20000
        MAXD = 3000
        csem = {}
        ccount = {}
        for e in self.CE:
            csem[e] = stack.enter_context(nc.semaphore("c_" + e))
            ccount[e] = 0
        dsems = {}
        duse = {}
        dcount = {}
        for q in ("sp", "pool", "bg"):
            dsems[q] = [stack.enter_context(nc.semaphore("d_%s_%d" % (q, i))) for i in range(NS)]
            duse[q] = [0] * NS
            dcount[q] = 0
        prevdma = {}
        for idx, o in enumerate(self.ops):
            if o.fn is None:
                continue
            if o.dma:
                q = o.q
                i = dcount[q] % NS
                dcount[q] += 1
                if duse[q][i] >= MAXD:
                    dsems[q][i] = stack.enter_context(nc.semaphore("d_%s_%d_%d" % (q, i, dcount[q])))
                    duse[q][i] = 0
                    prevdma[idx] = None
                else:
                    prevdma[idx] = (dsems[q][i], 16 * duse[q][i]) if duse[q][i] > 0 else None
                duse[q][i] += 1
                o.sem, o.val = dsems[q][i], 16 * duse[q][i]
            elif o.inc:
                e = o.eng
                if ccount[e] >= MAXC:
                    csem[e] = stack.enter_context(nc.semaphore("c_%s_%d" % (e, idx)))
                    ccount[e] = 0
                ccount[e] += 1
                o.sem, o.val = csem[e], ccount[e]
        ops = self.ops
        stream = self.stream
        snap = {}
        cur = {}
        curbg = {}
        for idx, o in enumerate(ops):
            if o.fn is not None and o.dma:
                if o.q == "bg":
                    curbg[id(o.sem)] = (o.sem, o.val)
                else:
                    cur[id(o.sem)] = (o.sem, o.val)
            if o.bar:
                snap[idx] = list(cur.values())
        final = list(cur.values()) + list(curbg.values())

        def run_engine(e, eng):
            waited = {}

            def wait(sem, val):
                k = id(sem)
                if waited.get(k, 0) >= val:
                    return
                waited[k] = val
                eng.wait_ge(sem, val)

            for idx in stream[e]:
                o = ops[idx]
                pend = []

                def want(sem, val):
                    k = id(sem)
                    if waited.get(k, 0) >= val:
                        return
                    waited[k] = val
                    for i_, (s_, v_) in enumerate(pend):
                        if s_ is sem:
                            pend[i_] = (s_, max(v_, val))
                            return
                    pend.append((sem, val))

                for d in o.deps:
                    od = ops[d]
                    want(od.sem, od.val)
                if o.bar:
                    for (s, v) in snap[idx]:
                        want(s, v)
                    for (s_, v_) in pend:
                        eng.wait_ge(s_, v_)
                    continue
                if o.dma and prevdma.get(idx) is not None:
                    want(*prevdma[idx])
                for (s_, v_) in pend[:-1]:
                    eng.wait_ge(s_, v_)
                ins = o.fn(eng)
                if pend:
                    ins._wait_ge(*pend[-1])
                if o.dma:
                    ins.then_inc(o.sem, 16)
                elif o.inc:
                    ins.then_inc(o.sem, 1)
            if e == "sp":
                for (s, v) in final:
                    wait(s, v)
                for ce in self.CE:
                    if ccount[ce] > 0:
                        wait(csem[ce], ccount[ce])

        with nc.Block() as block:
            @block.tensor
            def _(t):
                run_engine("pe", t)

            @block.scalar
            def _(a):
                run_engine("act", a)

            @block.vector
            def _(v):
                run_engine("dve", v)

            @block.gpsimd
            def _(g):
                run_engine("pool", g)

            @block.sync
            def _(s):
                run_engine("sp", s)


class Rot:
    def __init__(self, items):
        self.items = list(items)
        self.i = 0

    def next(self):
        v = self.items[self.i % len(self.items)]
        self.i += 1
        return v


W_SPECS = [
    ("w_in", 2048, 4096), ("wc_kv", 2048, 4096), ("w_out", 2048, 2048), ("wc_q", 2048, 2048),
    ("wc_o", 2048, 2048), ("w_up", 2048, 2 * D_FF), ("w_down", D_FF, 2048),
]


def ffn_tiles(S):
    half = S // 2
    tiles = []
    for hb in (0, half):
        t = 0
        while t < half:
            n = min(510, half - t)
            tiles.append((hb + t, n))
            t += n
    return tiles


def build_program(S, L, dbg=False):
    HALF = S // 2
    NT = S // 512
    NKT = S // 128
    nc = bass.Bass("TRN2", target_bir_lowering=False)
    P = Prog()
    stack = contextlib.ExitStack()

    def din(name, shape):
        return nc.dram_tensor(name, list(shape), F32, kind="ExternalInput").ap()

    def dint(name, shape, dt):
        return nc.dram_tensor(name, list(shape), dt, kind="Internal").ap()

    x_in = din("x", [S, D])
    mem_in = din("mem", [2, N_MEM, D])
    Wd = {}
    for name, K, N in W_SPECS:
        Wd[name] = din(name, [L, K, N])
    g_mix = din("g_mix", [L, D]); g_cross = din("g_cross", [L, D]); g_mem = din("g_mem", [L, D]); g_ffn = din("g_ffn", [L, D])
    g_q = din("g_q", [L, 64]); g_k = din("g_k", [L, 64])
    lam_in = [din(n, [L, 64]) for n in ("lam_q1", "lam_k1", "lam_q2", "lam_k2")]
    g_sub = din("g_sub", [L, 128])
    w_pool = din("w_pool", [L, 4, 256, 256])
    pool_scale = din("pool_scale", [L, 1024])
    gc_q = din("gc_q", [L, 512]); gc_k = din("gc_k", [L, 512])
    conv_w = din("conv_w", [L, 3, 2 * D_FF]); conv_b = din("conv_b", [L, 2 * D_FF])
    cos_in = din("rope_cos", [S, 32]); sin_in = din("rope_sin", [S, 32])
    invcnt_in = din("invcnt", [4, S])
    maskb_in = din("maskb", [128, 2]); bvec_in = din("bvec", [128, 1])
    ident_in = din("ident", [128, 128])
    y_out = nc.dram_tensor("y", [S, D], F32, kind="ExternalOutput").ap()

    Wt = {}
    for name, K, N in W_SPECS:
        kp = (K + 2047) // 2048
        Wt[name] = dint("wt_" + name, [L, kp * (N // 512), 128, 16, 512], BF16)
    xs = dint("xs", [S, D], F32)
    xmid = dint("xmid", [S, D], F32)
    QT = dint("QT", [NH, 128, S], BF16)
    KT = dint("KT", [NH, 128, S], BF16)
    Vd = dint("Vd", [S, 1024], BF16)
    UT = dint("UT", [1024, S + 16], F32)
    AT = dint("AT", [1024, S], BF16)
    HF = dint("HF", [2048, S + 2], BF16)
    CK = dint("CK", [2, 128, 16, 256], BF16)
    CV = dint("CV", [2, 128, 2, 2048], BF16)
    dbg_out = {}

    ARENA = 98000
    arena = stack.enter_context(nc.sbuf_tensor("arena", [128, ARENA], BF16))
    cst = stack.enter_context(nc.sbuf_tensor("cst", [128, 1200], F32))
    identb = stack.enter_context(nc.sbuf_tensor("identb", [128, 128], BF16))
    onesb = stack.enter_context(nc.sbuf_tensor("onesb", [128, 128], BF16))
    pp = [stack.enter_context(nc.psum_tensor("pp%d" % i, [128, 1024], F32)) for i in range(4)]
    psb = []
    for i in range(4):
        psb.append(pp[i][:, 0:512])
        psb.append(pp[i][:, 512:1024])

    class Arena:
        def __init__(self):
            self.off = 0

        def reset(self):
            self.off = 0

        def alloc(self, shape, dt):
            n = int(np.prod(shape[1:]))
            nb = n if dt == BF16 else 2 * n
            nb = (nb + 31) // 32 * 32
            a = arena[0:shape[0], self.off:self.off + nb]
            self.off += nb
            assert self.off <= ARENA, ("arena overflow", self.off)
            if dt == F32:
                a = a.bitcast(F32)
            a = a[:, 0:n]
            if len(shape) == 3:
                a = a.rearrange("p (a b) -> p a b", b=shape[2])
            elif len(shape) == 4:
                a = a.rearrange("p (a b c) -> p a b c", b=shape[2], c=shape[3])
            return a

    AR = Arena()

    c_off = [0]

    def calloc(n):
        a = cst[:, c_off[0]:c_off[0] + n]
        c_off[0] += n
        assert c_off[0] <= 1200
        return a

    maskb = calloc(2); bvec = calloc(1)
    gTall = calloc(128)
    gT = {"mix": gTall[:, 0:16], "cross": gTall[:, 16:32], "mem": gTall[:, 32:48], "ffn": gTall[:, 48:64]}
    gcqT = gTall[:, 64:68]; gckT = gTall[:, 68:72]
    pscale = gTall[:, 72:80]
    gsub_raw = gTall[:, 80:81]
    gq_rep = calloc(64); gk_rep = calloc(64)
    lamt = [calloc(64) for _ in range(4)]
    lamtmp = calloc(64); lamd = calloc(4); nlam = calloc(1); gsubs = calloc(1)
    convw = calloc(3 * 88); convb = calloc(88)
    smalls = calloc(64)
    identf = calloc(128)

    def psbf(b):
        return psb[b][:].bitcast(BF16)

    def dma(out, in_, r=(), w=(), q="sp", bg=False, slow=False):
        if slow:
            return P.op(q, lambda e: e.dma_start(out=out, in_=in_, allow_slow_non_contiguous=True), r=r, w=w, dma=True, bg=bg)
        return P.op(q, lambda e: e.dma_start(out=out, in_=in_), r=r, w=w, dma=True, bg=bg)

    def mm(out, lhsT, rhs, start, stop, r=(), w=()):
        return P.op("pe", lambda e: e.matmul(out, lhsT=lhsT, rhs=rhs, start=start, stop=stop), r=r, w=w)

    def tr(out, in_, r=(), w=()):
        return P.op("pe", lambda e: e.transpose(out, in_, identb[:]), r=r, w=w)

    def act(out, in_, func, r=(), w=(), bias=None, scale=None, accum_out=None):
        kw = {}
        if bias is not None:
            kw["bias"] = bias
        if scale is not None:
            kw["scale"] = scale
        if accum_out is not None:
            kw["accum_out"] = accum_out
        return P.op("act", lambda e: e.activation(out=out, in_=in_, func=func, **kw), r=r, w=w)

    class _Rec:
        def __getattr__(self, name):
            def f(*a, **kw):
                return (name, a, kw)
            return f

    _rec = _Rec()

    def vop(fn, r=(), w=(), eng="dve"):
        name, a, kw = fn(_rec)
        return P.op(eng, lambda e: getattr(e, name)(*a, **kw), r=r, w=w)

    def rstd_ops(dst, src, n, key):
        vop(lambda e: e.tensor_scalar(out=dst, in0=src, scalar1=1.0 / n, scalar2=EPS, op0=ALU.mult, op1=ALU.add), r=[key], w=[key])
        act(dst, dst, AF.Sqrt, r=[key], w=[key])
        vop(lambda e: e.reciprocal(out=dst, in_=dst), r=[key], w=[key])

    dma(cst[:, 0:2], maskb_in, w=["cst"])
    dma(bvec, bvec_in, w=["cst"])
    dma(identf, ident_in, w=["cst"])
    vop(lambda e: e.tensor_copy(out=identb[:], in_=identf), r=["cst"], w=["identb"])
    vop(lambda e: e.memset(onesb[:], 1.0), w=["onesb"])
    zt = AR.alloc([128, 64], F32)
    vop(lambda e: e.memset(zt, 0.0), w=["zt"])
    dma(UT[:, 0:8].rearrange("(c p) t -> p c t", p=128), zt.rearrange("p (c t) -> p c t", t=8), r=["zt"], w=["UTpad"])
    dma(UT[:, S + 8:S + 16].rearrange("(c p) t -> p c t", p=128), zt.rearrange("p (c t) -> p c t", t=8), r=["zt"], w=["UTpad"])
    ztb = AR.alloc([128, 32], BF16)
    vop(lambda e: e.memset(ztb, 0.0), w=["ztb"])
    dma(HF[:, 0:1].rearrange("(c p) t -> p c t", p=128), ztb[:, 0:16].rearrange("p (c t) -> p c t", t=1), r=["ztb"], w=["HFpad"], slow=True)
    dma(HF[:, S + 1:S + 2].rearrange("(c p) t -> p c t", p=128), ztb[:, 0:16].rearrange("p (c t) -> p c t", t=1), r=["ztb"], w=["HFpad"], slow=True)

    def wtiles(name):
        K, N = [(k, n) for (nm, k, n) in W_SPECS if nm == name][0]
        kp = (K + 2047) // 2048
        return kp, N // 512, K

    for l in range(L):
        for name, K, N in W_SPECS:
            kp, nn, _ = wtiles(name)
            for p_ in range(kp):
                k0 = p_ * 2048
                kc = min(16, (K - k0) // 128)
                for n in range(nn):
                    src = Wd[name][l, k0:k0 + kc * 128, n * 512:(n + 1) * 512].rearrange("(c p) j -> p c j", p=128)
                    dma(Wt[name][l, p_ * nn + n, :, 0:kc, :], src, w=[("W", name, l, p_ * nn + n)], q="pool", bg=True)
    P.barrier()

    def load_gT(dst, src_row):
        dma(dst, src_row.rearrange("(c p) -> p c", p=128), w=["gT"])

    def norm_transpose(xt_rows, rkeys, HT, htkey, col0, gTt, HB, hbkeys, ps_pool, nsub, ssq, rows=128):
        for s in range(nsub):
            xa = xt_rows(s)
            hb = HB[s % 2]
            hk = hbkeys[s % 2]
            sq = ssq[:, s:s + 1]
            act(hb, xa, AF.Square, r=[rkeys(s)], w=[hk, ("ssq", s)], accum_out=sq)
            rstd_ops(sq, sq, D, ("ssq", s))
            act(hb, xa, AF.Copy, r=[rkeys(s), ("ssq", s)], w=[hk], scale=sq)
            for half in range(2):
                b = ps_pool.next()
                pv = psbf(b).rearrange("p (c t) -> p c t", t=128)
                for c in range(8):
                    tr(pv[:, c, :], hb[:, (half * 8 + c) * 128:(half * 8 + c + 1) * 128], r=[hk], w=[("ps", b)])
                dst = HT[:, half * 8:half * 8 + 8, col0 + s * 128:col0 + (s + 1) * 128]
                g_b = gTt[:, half * 8:half * 8 + 8].unsqueeze(2).to_broadcast([128, 8, 128])
                vop(lambda e, dst=dst, pv=pv, g_b=g_b: e.tensor_tensor(out=dst, in0=pv, in1=g_b, op=ALU.mult),
                    r=[("ps", b), "gT"], w=[(htkey, s)])

    def load_w(WB, wkeys, name, l, tidx, kc=16):
        b, k = WB.next()
        dma(b[:, 0:kc, :], Wt[name][l, tidx, :, 0:kc, :], r=[("W", name, l, tidx)], w=[k])
        return b, k

    for l in range(L):
        lam_init = 0.8 - 0.6 * math.exp(-0.3 * l)
        xsrc = x_in if l == 0 else xs
        xdst = xs if l < L - 1 else y_out

        AR.reset()
        STG = AR.alloc([128, 128], F32)
        CSTG = AR.alloc([88, 4, 128], F32)
        vop(lambda e: e.memset(STG, 0.0), w=["STG"])
        for r0_, src_ in ((0, g_mix[l]), (16, g_cross[l]), (32, g_mem[l]), (48, g_ffn[l]), (64, gc_q[l]), (68, gc_k[l]), (72, pool_scale[l]), (80, g_sub[l])):
            nr = src_.shape[0] // 128
            dma(STG[r0_:r0_ + nr, :], src_.rearrange("(c p) -> c p", p=128), r=[], w=["STG"])
        P.op("pe", lambda e: e.transpose(psb[0][:, 0:128], STG, identf), r=["STG"], w=[("ps", 0)])
        act(gTall, psb[0][:, 0:128], AF.Copy, r=[("ps", 0)], w=["gT", "gsub", "pscale"])
        for j_ in range(3):
            dma(CSTG[:, j_, :], conv_w[l, j_].rearrange("(c p) -> c p", p=128), w=["CSTG"])
        dma(CSTG[:, 3, :], conv_b[l].rearrange("(c p) -> c p", p=128), w=["CSTG"])
        for j_ in range(4):
            P.op("pe", lambda e, j_=j_: e.transpose(psb[1][:, j_ * 88:(j_ + 1) * 88], CSTG[:, j_, :], identf[0:88, 0:88]), r=["CSTG"], w=[("ps", 1)])
        act(convw, psb[1][:, 0:264], AF.Copy, r=[("ps", 1)], w=["convw"])
        act(convb, psb[1][:, 264:352], AF.Copy, r=[("ps", 1)], w=["convw"])
        dma(gq_rep, g_q[l].partition_broadcast(128), w=["gq"])
        dma(gk_rep, g_k[l].partition_broadcast(128), w=["gq"])
        for i in range(4):
            dma(lamt[i], lam_in[i][l].partition_broadcast(128), w=["lam"])
        for i in range(2):
            vop(lambda e, i=i: e.tensor_tensor(out=lamtmp, in0=lamt[2 * i], in1=lamt[2 * i + 1], op=ALU.mult), r=["lam"], w=["lamtmp"])
            vop(lambda e, i=i: e.tensor_reduce(out=lamd[:, i:i + 1], in_=lamtmp, axis=AX.X, op=ALU.add), r=["lamtmp"], w=["lamd"])
        act(lamd[:, 2:4], lamd[:, 0:2], AF.Exp, r=["lamd"], w=["lamd"])
        vop(lambda e: e.tensor_tensor(out=nlam, in0=lamd[:, 3:4], in1=lamd[:, 2:3], op=ALU.subtract), r=["lamd"], w=["nlam"])
        vop(lambda e: e.tensor_scalar(out=nlam, in0=nlam, scalar1=-lam_init, scalar2=None, op0=ALU.add), r=["nlam"], w=["nlam"])
        vop(lambda e: e.tensor_scalar(out=gsubs, in0=gsub_raw, scalar1=1.0 - lam_init, scalar2=None, op0=ALU.mult), r=["gsub"], w=["gsubs"])
        P.barrier()

        AR.reset()
        WBl = [AR.alloc([128, 16, 512], BF16) for _ in range(3)]
        WB = Rot([(WBl[i], ("wb", i)) for i in range(3)])
        XR = AR.alloc([128, 4, D], F32)
        HB = [AR.alloc([128, D], BF16) for _ in range(2)]
        HT = AR.alloc([128, 16, 512], BF16)
        COS = AR.alloc([128, S // 128, 32], F32)
        SIN = AR.alloc([128, S // 128, 32], F32)
        SQ = [AR.alloc([128, 512], F32) for _ in range(2)]
        QN = [AR.alloc([128, 512], F32) for _ in range(2)]
        RT = [AR.alloc([128, 256], F32) for _ in range(4)]
        QR = [AR.alloc([128, 512], BF16) for _ in range(2)]
        QTs = AR.alloc([128, 16, 512], BF16)
        Vs = AR.alloc([128, 4, 1024], BF16)
        UTs = AR.alloc([128, 8, 512], F32)
        ssq = smalls[:, 0:4]
        qss = [smalls[:, 8:16], smalls[:, 16:24]]
        dma(COS, cos_in.rearrange("(s p) j -> p s j", p=128), w=["cos"])
        dma(SIN, sin_in.rearrange("(s p) j -> p s j", p=128), w=["cos"])
        psG = Rot([0, 1, 2, 3])
        psT = Rot([4, 5])
        psQ = Rot([6, 7])
        evi = 0
        pendA = []
        for ti in range(NT):
            t0 = ti * 512
            for s in range(4):
                dma(XR[:, s, :], xsrc[t0 + s * 128:t0 + (s + 1) * 128, :], w=[("XR", s)])
            norm_transpose(lambda s: XR[:, s, :], lambda s: ("XR", s), HT, "HT", 0, gT["mix"], HB, [("HB", 0), ("HB", 1)], psT, 4, ssq)
            htk = [("HT", s) for s in range(4)]
            for n in range(6):
                wb, wk = load_w(WB, None, "w_in", l, n)
                for s in range(4):
                    b = psG.next()
                    for k in range(16):
                        mm(psb[b][:], HT[:, k, s * 128:(s + 1) * 128], wb[:, k, :], k == 0, k == 15, r=[htk[s], wk], w=[("ps", b)])
                    while pendA:
                        pendA.pop(0)()
                    if n < 4:
                        e_ = evi % 2
                        evi += 1
                        sq_, qn_, qr_, qs_ = SQ[e_], QN[e_], QR[e_], qss[e_]
                        kq, kn, kr, ks = ("SQ", e_), ("QN", e_), ("QR", e_), ("qss", e_)
                        grep = gq_rep if n < 2 else gk_rep
                        act(sq_, psb[b][:], AF.Square, r=[("ps", b)], w=[kq])
                        vop(lambda e, sq_=sq_, qs_=qs_: e.tensor_reduce(out=qs_, in_=sq_.rearrange("p (g d) -> p g d", d=64), axis=AX.X, op=ALU.add), r=[kq], w=[ks])
                        rstd_ops(qs_, qs_, 64, ks)
                        pv3 = psb[b][:].rearrange("p (g d) -> p g d", d=64)
                        qn3 = qn_.rearrange("p (g d) -> p g d", d=64)
                        vop(lambda e, qn3=qn3, pv3=pv3, qs_=qs_: e.tensor_tensor(out=qn3, in0=pv3, in1=qs_.unsqueeze(2).to_broadcast([128, 8, 64]), op=ALU.mult),
                            r=[("ps", b), ks], w=[kn])
                        vop(lambda e, qn3=qn3, grep=grep: e.tensor_tensor(out=qn3, in0=qn3, in1=grep.unsqueeze(1).to_broadcast([128, 8, 64]), op=ALU.mult),
                            r=[kn, "gq"], w=[kn], eng="pool")
                        x1 = qn3[:, :, 0:32]
                        x2 = qn3[:, :, 32:64]
                        cs = COS[:, ti * 4 + s, :].unsqueeze(1).to_broadcast([128, 8, 32])
                        sn = SIN[:, ti * 4 + s, :].unsqueeze(1).to_broadcast([128, 8, 32])
                        r3 = [t.rearrange("p (g d) -> p g d", d=32) for t in RT]
                        qr3 = qr_.rearrange("p (g d) -> p g d", d=64)
                        vop(lambda e, x1=x1, cs=cs, r3=r3: e.tensor_tensor(out=r3[0], in0=x1, in1=cs, op=ALU.mult), r=[kn, "cos"], w=[("RT", 0)])
                        vop(lambda e, x2=x2, sn=sn, r3=r3: e.tensor_tensor(out=r3[1], in0=x2, in1=sn, op=ALU.mult), r=[kn, "cos"], w=[("RT", 1)])
                        vop(lambda e, r3=r3, qr3=qr3: e.tensor_tensor(out=qr3[:, :, 0:32], in0=r3[0], in1=r3[1], op=ALU.subtract), r=[("RT", 0), ("RT", 1)], w=[kr])
                        vop(lambda e, x2=x2, cs=cs, r3=r3: e.tensor_tensor(out=r3[2], in0=x2, in1=cs, op=ALU.mult), r=[kn, "cos"], w=[("RT", 2)], eng="pool")
                        vop(lambda e, x1=x1, sn=sn, r3=r3: e.tensor_tensor(out=r3[3], in0=x1, in1=sn, op=ALU.mult), r=[kn, "cos"], w=[("RT", 3)], eng="pool")
                        vop(lambda e, r3=r3, qr3=qr3: e.tensor_tensor(out=qr3[:, :, 32:64], in0=r3[2], in1=r3[3], op=ALU.add), r=[("RT", 2), ("RT", 3)], w=[kr], eng="pool")
                        def fin(qr_=qr_, kr=kr, n=n, s=s):
                            bq = psQ.next()
                            pq = psbf(bq).rearrange("p (c t) -> p c t", t=128)
                            for hh in range(4):
                                tr(pq[:, hh, :], qr_[:, hh * 128:(hh + 1) * 128], r=[kr], w=[("ps", bq)])
                            act(QTs[:, n * 4:n * 4 + 4, s * 128:(s + 1) * 128], pq[:, 0:4, :], AF.Copy, r=[("ps", bq)], w=[("QTs", n // 2)])
                        pendA.append(fin)
                    else:
                        act(Vs[:, s, (n - 4) * 512:(n - 3) * 512], psb[b][:], AF.Copy, r=[("ps", b)], w=["Vs"])
            while pendA:
                pendA.pop(0)()
            dma(QT[:, :, t0:t0 + 512].rearrange("h p t -> p h t"), QTs[:, 0:8, :], r=[("QTs", 0)], w=[("QT", ti)])
            dma(KT[:, :, t0:t0 + 512].rearrange("h p t -> p h t"), QTs[:, 8:16, :], r=[("QTs", 1)], w=[("KT", ti)])
            dma(Vd[t0:t0 + 512, :].rearrange("(s p) n -> p s n", p=128), Vs, r=["Vs"], w=[("Vd", ti)])
            for n in range(6, 8):
                wb, wk = load_w(WB, None, "w_in", l, n)
                for sc in range(4):
                    b = psG.next()
                    for k in range(16):
                        mm(psb[b][:], wb[:, k, sc * 128:(sc + 1) * 128], HT[:, k, :], k == 0, k == 15, r=htk + [wk], w=[("ps", b)])
                    act(UTs[:, (n - 6) * 4 + sc, :], psb[b][:], AF.Copy, r=[("ps", b)], w=["UTs"])
            dma(UT[:, 8 + t0:8 + t0 + 512].rearrange("(c p) t -> p c t", p=128), UTs, r=["UTs"], w=[("UT", ti)])
        P.barrier()

        AR.reset()
        KTb = [AR.alloc([128, S], BF16) for _ in range(2)]
        QTb = [AR.alloc([128, S], BF16) for _ in range(2)]
        Vb = [AR.alloc([128, NKT, 128], BF16) for _ in range(2)]
        Pt = [AR.alloc([128, 1024], BF16) for _ in range(3)]
        Pr = Rot([(Pt[i], ("P", i)) for i in range(3)])
        R0 = AR.alloc([128, 512], F32); R1 = AR.alloc([128, 512], F32)
        T0 = AR.alloc([128, 512], F32); T1 = AR.alloc([128, 512], F32)
        OC = AR.alloc([128, 512], F32); RS = AR.alloc([128, 512], F32)
        SQb = AR.alloc([128, 512], BF16)
        AO = [AR.alloc([128, 512], BF16) for _ in range(2)]
        psS = Rot([0, 1])
        scale = 64 ** -0.5
        it = 0

        def head_loads(h):
            hb_ = h % 2
            dma(KTb[hb_], KT[h], r=[("KT", ti) for ti in range(NT)], w=[("KTb", hb_)])
            dma(QTb[hb_], QT[h], r=[("QT", ti) for ti in range(NT)], w=[("QTb", hb_)])
            dma(Vb[hb_], Vd[:, h * 128:(h + 1) * 128].rearrange("(kt p) d -> p kt d", p=128), r=[("Vd", ti) for ti in range(NT)], w=[("Vb", hb_)])

        def issue_qk(step):
            h, qc, kt = step
            hb_ = h % 2
            q0 = qc * 512
            pi = psS.next()
            for c in range(2):
                mm(psb[2 * pi + c][:], KTb[hb_][c * 64:(c + 1) * 64, kt * 128:(kt + 1) * 128], QTb[hb_][c * 64:(c + 1) * 64, q0:q0 + 512], True, True,
                   r=[("KTb", hb_), ("QTb", hb_)], w=[("pp", pi)])
            return pi

        def epilogue2(h, q0, qc):
            nonlocal_it = it_box[0]
            pi = psS.next()
            psS.next()
            mm(psb[2 * pi][:], onesb[:], SQb, True, True, r=["SQb"], w=[("pp", pi)])
            vop(lambda e: e.tensor_scalar(out=RS, in0=psb[2 * pi][:], scalar1=1.0 / 128, scalar2=EPS, op0=ALU.mult, op1=ALU.add), r=[("pp", pi)], w=["RS"])
            act(RS, RS, AF.Sqrt, r=["RS"], w=["RS"])
            vop(lambda e: e.reciprocal(out=RS, in_=RS), r=["RS"], w=["RS"])
            ao = AO[nonlocal_it % 2]
            ak = ("AO", nonlocal_it % 2)
            it_box[0] += 1
            vop(lambda e: e.scalar_tensor_tensor(out=ao, in0=OC, scalar=gsubs, in1=RS, op0=ALU.mult, op1=ALU.mult), r=["OC", "RS"], w=[ak])
            dma(AT[h * 128:(h + 1) * 128, q0:q0 + 512], ao, r=[ak], w=[("AT", qc)])

        it_box = [0]
        steps = [(h, qc, kt) for h in range(NH) for qc in range(NT) for kt in range(NKT)]
        deferred = []
        head_loads(0)
        pi_cur = issue_qk(steps[0])
        for i, (h, qc, kt) in enumerate(steps):
            hb_ = h % 2
            q0 = qc * 512
            if qc == 0 and kt == 0 and h + 1 < NH:
                head_loads(h + 1)
            pi_next = issue_qk(steps[i + 1]) if i + 1 < len(steps) else None
            qhalf = 0 if q0 < HALF else 1
            khalf = 0 if kt * 128 < HALF else 1
            mb = maskb[:, 0:1] if khalf == qhalf else maskb[:, 1:2]
            pt, pk = Pr.next()
            act(pt, pp[pi_cur][:], AF.Exp, r=[("pp", pi_cur)], w=[pk], bias=mb, scale=scale)
            for c in range(2):
                mm(psb[4 + c][:], Vb[hb_][:, kt, :], pt[:, c * 512:(c + 1) * 512], kt == 0, kt == NKT - 1, r=[("Vb", hb_), pk], w=[("ps", 4 + c)])
                mm(psb[6 + c][:], onesb[:], pt[:, c * 512:(c + 1) * 512], kt == 0, kt == NKT - 1, r=[pk], w=[("ps", 6 + c)])
            while deferred and deferred[0][0] <= i:
                deferred.pop(0)[1]()
            if kt == NKT - 1:
                vop(lambda e: e.tensor_copy(out=T0, in_=psb[4][:]), r=[("ps", 4)], w=["T0"])
                act(R0, psb[6][:], AF.Copy, r=[("ps", 6)], w=["R0"])
                vop(lambda e: e.tensor_copy(out=T1, in_=psb[5][:]), r=[("ps", 5)], w=["T1"])
                act(R1, psb[7][:], AF.Copy, r=[("ps", 7)], w=["R1"])
                vop(lambda e: e.reciprocal(out=R0, in_=R0), r=["R0"], w=["R0"])
                vop(lambda e: e.reciprocal(out=R1, in_=R1), r=["R1"], w=["R1"])
                vop(lambda e: e.tensor_tensor(out=T0, in0=T0, in1=R0, op=ALU.mult), r=["T0", "R0"], w=["T0"])
                vop(lambda e: e.tensor_tensor(out=T1, in0=T1, in1=R1, op=ALU.mult), r=["T1", "R1"], w=["T1"])
                vop(lambda e: e.scalar_tensor_tensor(out=OC, in0=T1, scalar=nlam, in1=T0, op0=ALU.mult, op1=ALU.add), r=["T0", "T1"], w=["OC"])
                act(SQb, OC, AF.Square, r=["OC"], w=["SQb"])
                deferred.append((i + 8, (lambda h=h, q0=q0, qc=qc: epilogue2(h, q0, qc))))
            pi_cur = pi_next
        while deferred:
            deferred.pop(0)[1]()
        P.barrier()

        AR.reset()
        WBl = [AR.alloc([128, 16, 512], BF16) for _ in range(3)]
        WB = Rot([(WBl[i], ("wb", i)) for i in range(3)])
        XR = AR.alloc([128, 4, D], F32)
        HB = [AR.alloc([128, D], BF16) for _ in range(2)]
        HTa = AR.alloc([128, 16, 512], BF16)
        HTb = AR.alloc([128, 16, 512], BF16)
        CKs = AR.alloc([128, 16, 256], BF16)
        CVs = AR.alloc([128, 2, 2048], BF16)
        UTt = [AR.alloc([128, 2, 528], F32) for _ in range(2)]
        PW = [AR.alloc([128, 2, 528], F32) for _ in range(2)]
        ZT = [AR.alloc([128, 2, 512], BF16) for _ in range(2)]
        ICN = AR.alloc([128, 4, 512], F32)
        WP = AR.alloc([128, 8, 256], BF16)
        CQN = [AR.alloc([128, 512], BF16) for _ in range(2)]
        CQT = AR.alloc([128, 4, 512], BF16)
        PC = [AR.alloc([128, 512], BF16) for _ in range(4)]
        RC = AR.alloc([128, 512], F32)
        ssq = smalls[:, 0:4]
        cqs = [smalls[:, 8:9], smalls[:, 9:10]]
        psG = Rot([0, 1, 2, 3])
        psT = Rot([4, 5])
        psQ = Rot([6, 7])
        dma(WP, w_pool[l].rearrange("g (cc p) d -> p (g cc) d", p=128), w=["WP"], q="pool")
        for hf in range(2):
            for s in range(2):
                dma(XR[:, s, :], mem_in[hf, s * 128:(s + 1) * 128, :], w=[("XR", s)])
            norm_transpose(lambda s: XR[:, s, :], lambda s: ("XR", s), HTa, "HTa", 0, gT["mem"], HB, [("HB", 0), ("HB", 1)], psT, 2, ssq)
            hk = [("HTa", 0), ("HTa", 1)]
            for n in range(8):
                wb, wk = load_w(WB, None, "wc_kv", l, n)
                for s in range(2):
                    b = psG.next()
                    for k in range(16):
                        mm(psb[b][:], HTa[:, k, s * 128:(s + 1) * 128], wb[:, k, :], k == 0, k == 15, r=[hk[s], wk], w=[("ps", b)])
                    if n < 4:
                        e_ = (n * 2 + s) % 2
                        act(CQN[e_], psb[b][:], AF.Square, r=[("ps", b)], w=[("CQN", e_), ("cqs", e_)], accum_out=cqs[e_])
                        rstd_ops(cqs[e_], cqs[e_], 512, ("cqs", e_))
                        act(CQN[e_], psb[b][:], AF.Copy, r=[("ps", b), ("cqs", e_)], w=[("CQN", e_)], scale=cqs[e_])
                        bq = psQ.next()
                        pq = psbf(bq).rearrange("p (c t) -> p c t", t=128)
                        for dc in range(4):
                            tr(pq[:, dc, :], CQN[e_][:, dc * 128:(dc + 1) * 128], r=[("CQN", e_)], w=[("ps", bq)])
                        dstk = CKs[:, n * 4:n * 4 + 4, s * 128:(s + 1) * 128]
                        g_b = gckT[:, 0:4].unsqueeze(2).to_broadcast([128, 4, 128])
                        vop(lambda e, dstk=dstk, pq=pq, g_b=g_b: e.tensor_tensor(out=dstk, in0=pq[:, 0:4, :], in1=g_b, op=ALU.mult), r=[("ps", bq), "gT"], w=["CKs"])
                    else:
                        act(CVs[:, s, (n - 4) * 512:(n - 3) * 512], psb[b][:], AF.Copy, r=[("ps", b)], w=["CVs"])
            dma(CK[hf], CKs, r=["CKs"], w=[("CK", hf)])
            dma(CV[hf], CVs, r=["CVs"], w=[("CV", hf)])
        for ti in range(NT):
            t0 = ti * 512
            hf = 0 if t0 < HALF else 1
            for s in range(4):
                dma(XR[:, s, :], xsrc[t0 + s * 128:t0 + (s + 1) * 128, :], w=[("XR", s)])
            dma(CKs, CK[hf], r=[("CK", hf)], w=["CKs"])
            dma(CVs, CV[hf], r=[("CV", hf)], w=["CVs"])
            dma(ICN, invcnt_in[:, t0:t0 + 512].partition_broadcast(128), w=["ICN"])
            dma(HTa[:, 0:8, :], AT[:, t0:t0 + 512].rearrange("(c p) t -> p c t", p=128), r=[("AT", ti)], w=[("HTa", c) for c in range(8)])
            for g in range(4):
                wdw = 2 << g
                u = UTt[g % 2]
                uk = ("UTt", g % 2)
                dma(u, UT[g * 256:(g + 1) * 256, t0:t0 + 528].rearrange("(c p) t -> p c t", p=128),
                    r=[("UT", tj) for tj in range(max(0, ti - 1), min(NT, ti + 2))] + ["UTpad"], w=[uk])
                if t0 + 512 == HALF:
                    vop(lambda e, u=u: e.tensor_scalar(out=u[:, :, 520:528], in0=u[:, :, 520:528], scalar1=bvec, scalar2=None, op0=ALU.mult), r=[uk], w=[uk])
                if t0 == HALF:
                    vop(lambda e, u=u: e.tensor_scalar(out=u[:, :, 0:8], in0=u[:, :, 0:8], scalar1=bvec, scalar2=None, op0=ALU.mult), r=[uk], w=[uk])
                a_, b_ = PW[0], PW[1]
                vop(lambda e, u=u, a_=a_: e.tensor_tensor(out=a_[:, :, 1:528], in0=u[:, :, 0:527], in1=u[:, :, 1:528], op=ALU.add), r=[uk], w=["PW0"])
                cur, curk, oth, othk = a_, "PW0", b_, "PW1"
                lo, hi = 1, 528
                sh = 1
                for step in range(g):
                    nlo, nhi = lo + sh, hi - sh
                    vop(lambda e, cur=cur, oth=oth, nlo=nlo, nhi=nhi, sh=sh: e.tensor_tensor(out=oth[:, :, nlo:nhi], in0=cur[:, :, nlo - sh:nhi - sh], in1=cur[:, :, nlo + sh:nhi + sh], op=ALU.add),
                        r=[curk], w=[othk])
                    cur, curk, oth, othk = oth, othk, cur, curk
                    lo, hi = nlo, nhi
                    sh *= 2
                assert lo <= 8 and hi >= 520
                icb = ICN[:, g, :].unsqueeze(1).to_broadcast([128, 2, 512])
                vop(lambda e, cur=cur, oth=oth, icb=icb: e.tensor_tensor(out=oth[:, :, 8:520], in0=cur[:, :, 8:520], in1=icb, op=ALU.mult), r=[curk, "ICN"], w=[othk])
                z = ZT[g % 2]
                zk = ("ZT", g % 2)
                vop(lambda e, oth=oth, u=u, z=z: e.tensor_tensor(out=z, in0=oth[:, :, 8:520], in1=u[:, :, 8:520], op=ALU.subtract), r=[othk, uk], w=[zk])
                for dd in range(2):
                    b = psQ.next()
                    for cc in range(2):
                        mm(psb[b][:], WP[:, g * 2 + cc, dd * 128:(dd + 1) * 128], z[:, cc, :], cc == 0, cc == 1, r=[zk, "WP"], w=[("ps", b)])
                    act(HTa[:, 8 + 2 * g + dd, :], psb[b][:], AF.Copy, r=[("ps", b), "pscale"], w=[("HTa", 8 + 2 * g + dd)], scale=pscale[:, 2 * g + dd:2 * g + dd + 1])
            mk = [("HTa", i) for i in range(16)]
            for n in range(4):
                wb, wk = load_w(WB, None, "w_out", l, n)
                for s in range(4):
                    b = psG.next()
                    for k in range(16):
                        mm(psb[b][:], HTa[:, k, s * 128:(s + 1) * 128], wb[:, k, :], k == 0, k == 15, r=mk + [wk], w=[("ps", b)])
                    xa = XR[:, s, n * 512:(n + 1) * 512]
                    vop(lambda e, xa=xa, b=b: e.tensor_tensor(out=xa, in0=psb[b][:], in1=xa, op=ALU.add), r=[("ps", b), ("XR", s)], w=[("XR", s)])
            norm_transpose(lambda s: XR[:, s, :], lambda s: ("XR", s), HTb, "HTb", 0, gT["cross"], HB, [("HB", 0), ("HB", 1)], psT, 4, ssq)
            hbk = [("HTb", s) for s in range(4)]
            ei = 0
            pendC = []
            for hh in range(4):
                wb, wk = load_w(WB, None, "wc_q", l, hh)
                for s in range(4):
                    b = psG.next()
                    for k in range(16):
                        mm(psb[b][:], HTb[:, k, s * 128:(s + 1) * 128], wb[:, k, :], k == 0, k == 15, r=[hbk[s], wk], w=[("ps", b)])
                    while pendC:
                        pendC.pop(0)()
                    e_ = ei % 2
                    ei += 1
                    act(CQN[e_], psb[b][:], AF.Square, r=[("ps", b)], w=[("CQN", e_), ("cqs", e_)], accum_out=cqs[e_])
                    rstd_ops(cqs[e_], cqs[e_], 512, ("cqs", e_))
                    act(CQN[e_], psb[b][:], AF.Copy, r=[("ps", b), ("cqs", e_)], w=[("CQN", e_)], scale=cqs[e_])
                    def finq(e_=e_, s=s):
                        bq = psQ.next()
                        pq = psbf(bq).rearrange("p (c t) -> p c t", t=128)
                        for dc in range(4):
                            tr(pq[:, dc, :], CQN[e_][:, dc * 128:(dc + 1) * 128], r=[("CQN", e_)], w=[("ps", bq)])
                        dstq = CQT[:, 0:4, s * 128:(s + 1) * 128]
                        g_b = gcqT[:, 0:4].unsqueeze(2).to_broadcast([128, 4, 128])
                        vop(lambda e: e.tensor_tensor(out=dstq, in0=pq[:, 0:4, :], in1=g_b, op=ALU.mult), r=[("ps", bq), "gT"], w=["CQT"])
                    pendC.append(finq)
                while pendC:
                    pendC.pop(0)()
                pcs = []
                for mc in range(2):
                    b = psG.next()
                    for dc in range(4):
                        mm(psb[b][:], CKs[:, hh * 4 + dc, mc * 128:(mc + 1) * 128], CQT[:, dc, :], dc == 0, dc == 3, r=["CKs", "CQT"], w=[("ps", b)])
                    pc = PC[(hh * 2 + mc) % 4]
                    pk = ("PC", (hh * 2 + mc) % 4)
                    act(pc, psb[b][:], AF.Exp, r=[("ps", b)], w=[pk], scale=512 ** -0.5)
                    pcs.append((pc, pk))
                bz = psG.next()
                for mc in range(2):
                    mm(psb[bz][:], onesb[:], pcs[mc][0], mc == 0, mc == 1, r=[pcs[mc][1]], w=[("ps", bz)])
                vop(lambda e, bz=bz: e.reciprocal(out=RC, in_=psb[bz][:]), r=[("ps", bz)], w=["RC"])
                for dvc in range(4):
                    b = psG.next()
                    for mc in range(2):
                        mm(psb[b][:], CVs[:, mc, hh * 512 + dvc * 128:hh * 512 + (dvc + 1) * 128], pcs[mc][0], mc == 0, mc == 1, r=["CVs", pcs[mc][1]], w=[("ps", b)])
                    dsto = HTa[:, hh * 4 + dvc, :]
                    vop(lambda e, dsto=dsto, b=b: e.tensor_tensor(out=dsto, in0=psb[b][:], in1=RC, op=ALU.mult), r=[("ps", b), "RC"], w=[("HTa", hh * 4 + dvc)])
            ok = mk
            for n in range(4):
                wb, wk = load_w(WB, None, "wc_o", l, n)
                for s in range(4):
                    b = psG.next()
                    for k in range(16):
                        mm(psb[b][:], HTa[:, k, s * 128:(s + 1) * 128], wb[:, k, :], k == 0, k == 15, r=ok + [wk], w=[("ps", b)])
                    xa = XR[:, s, n * 512:(n + 1) * 512]
                    vop(lambda e, xa=xa, b=b: e.tensor_tensor(out=xa, in0=psb[b][:], in1=xa, op=ALU.add), r=[("ps", b), ("XR", s)], w=[("XR", s)])
            for s in range(4):
                dma(xmid[t0 + s * 128:t0 + (s + 1) * 128, :], XR[:, s, :], r=[("XR", s)], w=[("xmid", ti)])
            norm_transpose(lambda s: XR[:, s, :], lambda s: ("XR", s), HTb, "HTb", 0, gT["ffn"], HB, [("HB", 0), ("HB", 1)], psT, 4, ssq)
            dma(HF[:, 1 + t0:1 + t0 + 512].rearrange("(c p) t -> p c t", p=128), HTb, r=hbk, w=[("HF", ti)])
        P.barrier()

        AR.reset()
        WBl = [AR.alloc([128, 16, 512], BF16) for _ in range(4)]
        WB = Rot([(WBl[i], ("wb", i)) for i in range(4)])
        HTf = [AR.alloc([128, 16, 512], BF16) for _ in range(2)]
        ACTT = AR.alloc([128, 44, 512], BF16)
        XF = [AR.alloc([128, D], F32) for _ in range(4)]
        CA = [AR.alloc([128, 512], F32) for _ in range(4)]
        GL = [AR.alloc([128, 512], F32) for _ in range(2)]
        psU = Rot([0, 1, 2, 3])
        psD = [4, 5, 6, 7]
        hfk_all = [("HF", tj) for tj in range(NT)] + ["HFpad"]
        xi = 0
        for fi, (ts, tn) in enumerate(ffn_tiles(S)):
            wcols = tn + 2
            ht = HTf[fi % 2]
            hk_ = ("HTf", fi % 2)
            dma(ht[:, :, 0:wcols], HF[:, ts:ts + wcols].rearrange("(c p) t -> p c t", p=128), r=hfk_all, w=[hk_])
            if ts + tn == HALF:
                vop(lambda e, ht=ht, wcols=wcols: e.tensor_scalar(out=ht[:, :, wcols - 1:wcols], in0=ht[:, :, wcols - 1:wcols], scalar1=bvec, scalar2=None, op0=ALU.mult), r=[hk_], w=[hk_])
            if ts == HALF:
                vop(lambda e, ht=ht: e.tensor_scalar(out=ht[:, :, 0:1], in0=ht[:, :, 0:1], scalar1=bvec, scalar2=None, op0=ALU.mult), r=[hk_], w=[hk_])
            for i in range(11):
                wg, wgk = load_w(WB, None, "w_up", l, i)
                wv, wvk = load_w(WB, None, "w_up", l, 11 + i)
                for sc in range(4):
                    j = i * 4 + sc
                    res = []
                    for which, (wb, wk) in enumerate(((wg, wgk), (wv, wvk))):
                        b = psU.next()
                        for k in range(16):
                            mm(psb[b][:, 0:wcols], wb[:, k, sc * 128:(sc + 1) * 128], ht[:, k, 0:wcols], k == 0, k == 15, r=[hk_, wk], w=[("ps", b)])
                        ch = j if which == 0 else 44 + j
                        ca = CA[which * 2 + (j % 2)]
                        cak = ("CA", which * 2 + (j % 2))
                        act(ca[:, 0:tn], psb[b][:, 1:1 + tn], AF.Identity, r=[("ps", b), "convw"], w=[cak],
                            bias=convb[:, ch:ch + 1], scale=convw[:, 88 + ch:88 + ch + 1])
                        vop(lambda e, ca=ca, b=b, ch=ch: e.scalar_tensor_tensor(out=ca[:, 0:tn], in0=psb[b][:, 0:tn], scalar=convw[:, ch:ch + 1], in1=ca[:, 0:tn], op0=ALU.mult, op1=ALU.add),
                            r=[("ps", b), cak], w=[cak])
                        vop(lambda e, ca=ca, b=b, ch=ch: e.scalar_tensor_tensor(out=ca[:, 0:tn], in0=psb[b][:, 2:2 + tn], scalar=convw[:, 176 + ch:176 + ch + 1], in1=ca[:, 0:tn], op0=ALU.mult, op1=ALU.add),
                            r=[("ps", b), cak], w=[cak])
                        res.append((ca, cak))
                    gl = GL[j % 2]
                    glk = ("GL", j % 2)
                    act(gl[:, 0:tn], res[0][0][:, 0:tn], AF.Gelu_apprx_tanh, r=[res[0][1]], w=[glk])
                    vop(lambda e, gl=gl, cv_=res[1][0], j=j: e.tensor_tensor(out=ACTT[:, j, 0:tn], in0=gl[:, 0:tn], in1=cv_[:, 0:tn], op=ALU.mult),
                        r=[glk, res[1][1]], w=[("ACTT", j)], eng="pool")
            ak_all = [("ACTT", j) for j in range(44)]
            subs = []
            o = 0
            while o < tn:
                subs.append((o, min(128, tn - o)))
                o += 128
            for n in range(4):
                for p_ in range(3):
                    kc = 16 if p_ < 2 else 12
                    wb, wk = load_w(WB, None, "w_down", l, p_ * 4 + n, kc=kc)
                    for si, (so, sn_) in enumerate(subs):
                        b = psD[si]
                        for k in range(kc):
                            kk = p_ * 16 + k
                            mm(psb[b][0:sn_, :], ACTT[:, kk, so:so + sn_], wb[:, k, :], kk == 0, kk == 43, r=ak_all + [wk], w=[("ps", b)])
                for si, (so, sn_) in enumerate(subs):
                    b = psD[si]
                    if n == 0:
                        pass
                    xf = XF[si]
                    xfk = ("XF", si)
                    if n == 0:
                        dma(xf[0:sn_, :], xmid[ts + so:ts + so + sn_, :], r=[("xmid", tj) for tj in range(NT)], w=[xfk])
                    xa = xf[0:sn_, n * 512:(n + 1) * 512]
                    vop(lambda e, xa=xa, b=b, sn_=sn_: e.tensor_tensor(out=xa, in0=psb[b][0:sn_, :], in1=xa, op=ALU.add), r=[("ps", b), xfk], w=[xfk])
                    if n == 3:
                        dma(xdst[ts + so:ts + so + sn_, :], xf[0:sn_, :], r=[xfk], w=[("xout", fi, si)])
            xi += len(subs)
        P.barrier()

    P.emit(nc, stack)
    stack.close()
    return nc


def core_tables(S, split):
    half = S // 2
    pos = np.arange(S)
    if split:
        pos = pos % half
        seqlen = half
    else:
        seqlen = S
    inv = (10000.0 ** (-np.arange(32, dtype=np.float32) / 32)).astype(np.float32)
    ang = pos.astype(np.float32)[:, None] * inv[None, :]
    cos = np.cos(ang).astype(np.float32)
    sin = np.sin(ang).astype(np.float32)
    invcnt = np.zeros((4, S), np.float32)
    for g, w in enumerate((2, 4, 8, 16)):
        lo = np.clip(pos - w // 2, 0, seqlen)
        hi = np.clip(pos + w // 2, 0, seqlen)
        invcnt[g] = 1.0 / (hi - lo).astype(np.float32)
    maskb = np.zeros((128, 2), np.float32)
    maskb[:, 1] = NEG if split else 0.0
    bvec = np.full((128, 1), 0.0 if split else 1.0, np.float32)
    return cos, sin, invcnt, maskb, bvec


_NC_CACHE = {}


def run_cores(core_specs, weights, S, L):
    key = (S, L)
    if key not in _NC_CACHE:
        _NC_CACHE[key] = build_program(S, L)
    nc = _NC_CACHE[key]
    ident = np.eye(128, dtype=np.float32)
    in_maps = []
    for x, mem, split in core_specs:
        cos, sin, invcnt, maskb, bvec = core_tables(S, split)
        m = {"x": np.ascontiguousarray(x, dtype=np.float32), "mem": np.ascontiguousarray(mem, dtype=np.float32),
             "rope_cos": cos, "rope_sin": sin, "invcnt": invcnt, "maskb": maskb, "bvec": bvec, "ident": ident}
        for k, v in weights.items():
            m[k] = np.ascontiguousarray(v, dtype=np.float32)
        in_maps.append(m)
    res = run_bass_kernel_spmd(nc, in_maps, core_ids=list(range(len(core_specs))))
    return [r["y"] for r in res.results]


WEIGHT_NAMES = ["g_mix", "w_in", "g_q", "g_k", "lam_q1", "lam_k1", "lam_q2", "lam_k2", "g_sub", "w_pool", "pool_scale",
                "w_out", "g_cross", "g_mem", "wc_q", "wc_kv", "gc_q", "gc_k", "wc_o", "g_ffn", "w_up", "conv_w", "conv_b", "w_down"]


def kernel(**inputs):
    xp = np.asarray(inputs["x_prompt"]); xsm = np.asarray(inputs["x_sample"])
    mp = np.asarray(inputs["mem_prompt"]); ms = np.asarray(inputs["mem_sample"])
    weights = {k: np.asarray(inputs[k]) for k in WEIGHT_NAMES}
    L = weights["w_in"].shape[0]
    S = xp.shape[1]
    specs = []
    for i in range(4):
        specs.append((xp[i], np.stack([mp[i], mp[i]]), False))
    for j in range(4):
        specs.append((xsm[2 * j:2 * j + 2].reshape(S, D), ms[2 * j:2 * j + 2], True))
    ys = run_cores(specs, weights, S, L)
    y_prompt = np.stack(ys[0:4]).astype(np.float32)
    y_sample = np.concatenate([y.reshape(2, S // 2, D) for y in ys[4:8]], axis=0).astype(np.float32)
    return (y_prompt, y_sample)
```
